# Optimizing a Trainium2 kernel written in Bass

```python
import jax, jax.numpy as jnp
from jax import lax
import numpy as np

D_MODEL = 2048
BATCH = 4
SEQ = 2048
DEPTH = 2

HEAD_DIM = 64
Q_BLOCK = 128
ROPE_THETA = 10000.0
EPS = 1e-6

N_GROUPS = 4
GROUP_WIDTH = D_MODEL // N_GROUPS
MIX_WIDTH = N_GROUPS * GROUP_WIDTH

SB_HEADS = GROUP_WIDTH // HEAD_DIM
CONV_CH = GROUP_WIDTH
CONV_WIDTH = 31
CONV_LN_EPS = 1e-5
NSA_HEADS = GROUP_WIDTH // HEAD_DIM
NSA_KV_HEADS = 2
NSA_REP = NSA_HEADS // NSA_KV_HEADS
NSA_KV_COLS = NSA_KV_HEADS * HEAD_DIM
NSA_CMP_LEN = 32
NSA_CMP_STRIDE = 16
NSA_SEL_LEN = 64
NSA_N_SEL = 16
NSA_WINDOW = 512
NSA_FORCE_BONUS = 1e3
MLA_HEADS = 8
MLA_Q_RANK = 384
MLA_KV_RANK = 256
MLA_NOPE_DIM = 64
MLA_ROPE_DIM = 32
MLA_QK_DIM = MLA_NOPE_DIM + MLA_ROPE_DIM
MLA_V_DIM = GROUP_WIDTH // MLA_HEADS
D_FF = 5632
FFN_CONV_WIDTH = 3

SB_COLS = 3 * SB_HEADS * HEAD_DIM
CONV_COLS = 2 * CONV_CH
NSA_Q_COLS = NSA_HEADS * HEAD_DIM
NSA_COLS = NSA_Q_COLS + 6 * NSA_KV_COLS + 3 * NSA_HEADS
MLA_COLS = MLA_Q_RANK + MLA_KV_RANK + MLA_ROPE_DIM
IN_COLS = SB_COLS + CONV_COLS + NSA_COLS + MLA_COLS
OFF_CONV = SB_COLS
OFF_NSA = OFF_CONV + CONV_COLS
OFF_MLA = OFF_NSA + NSA_COLS

kernel_name = 'hybrid_sb_conformer_nsa_mla_block'


def rms_norm(x, g, eps=EPS):
    x32 = x.astype(jnp.float32)
    y = x32 * lax.rsqrt(jnp.mean(x32 * x32, axis=-1, keepdims=True) + eps)
    return (y * g.astype(jnp.float32)).astype(x.dtype)


def rope(x, pos):
    d = x.shape[-1]
    inv = ROPE_THETA ** (-jnp.arange(0, d, 2, dtype=jnp.float32) / d)
    ang = pos.astype(jnp.float32)[:, None] * inv[None, :]
    cos, sin = jnp.cos(ang), jnp.sin(ang)
    x32 = x.astype(jnp.float32)
    x1, x2 = x32[..., : d // 2], x32[..., d // 2:]
    return jnp.concatenate([x1 * cos - x2 * sin, x1 * sin + x2 * cos], -1).astype(x.dtype)


def causal_dwconv(x, w, b):
    k_w, ch = w.shape
    y = lax.conv_general_dilated(x, w[:, None, :].astype(x.dtype), window_strides=(1,),
                                 padding=[(k_w - 1, 0)], dimension_numbers=('NWC', 'WIO', 'NWC'),
                                 feature_group_count=ch)
    return y + b


def masked_softmax(s, mask):
    s = jnp.where(mask, s.astype(jnp.float32), -jnp.inf)
    m = jnp.max(s, axis=-1, keepdims=True)
    m = jnp.where(jnp.isfinite(m), m, 0.0)
    p = jnp.exp(s - m)
    return p / jnp.maximum(jnp.sum(p, axis=-1, keepdims=True), 1e-30)


def stick_breaking_attention(q, k, v):
    B, H, T, d = q.shape
    nq = T // Q_BLOCK
    scale = d ** -0.5
    kpos = jnp.arange(T)
    qb = q.reshape(B, H, nq, Q_BLOCK, d).transpose(2, 0, 1, 3, 4)
    tb = kpos.reshape(nq, Q_BLOCK)

    def block(args):
        qi, ti = args
        z = jnp.einsum('bhqd,bhkd->bhqk', qi, k).astype(jnp.float32) * scale
        mask = kpos[None, :] < ti[:, None]
        log_beta = jax.nn.log_sigmoid(z)
        log_1m = jnp.where(mask, jax.nn.log_sigmoid(-z), 0.0)
        after = lax.cumsum(log_1m, axis=3, reverse=True) - log_1m
        a = jnp.where(mask, jnp.exp(log_beta + after), 0.0)
        return jnp.einsum('bhqk,bhkd->bhqd', a.astype(v.dtype), v)

    o = lax.map(block, (qb, tb))
    return o.transpose(1, 2, 0, 3, 4).reshape(B, H, T, d)


def causal_attention(q, k, v):
    B, H, T, dk = q.shape
    nq = T // Q_BLOCK
    scale = dk ** -0.5
    kpos = jnp.arange(T)
    qb = q.reshape(B, H, nq, Q_BLOCK, dk).transpose(2, 0, 1, 3, 4)
    tb = kpos.reshape(nq, Q_BLOCK)

    def block(args):
        qi, ti = args
        s = jnp.einsum('bhqd,bhkd->bhqk', qi, k) * scale
        p = masked_softmax(s, kpos[None, :] <= ti[:, None])
        return jnp.einsum('bhqk,bhkd->bhqd', p.astype(v.dtype), v)

    o = lax.map(block, (qb, tb))
    return o.transpose(1, 2, 0, 3, 4).reshape(B, H, T, v.shape[-1])


def conformer_conv(u, dw_w, dw_b, ln_g, ln_b, pw_w, pw_b):
    a, g = jnp.split(u, 2, axis=-1)
    h = causal_dwconv(a * jax.nn.sigmoid(g), dw_w, dw_b)
    h32 = h.astype(jnp.float32)
    mu = jnp.mean(h32, axis=-1, keepdims=True)
    var = jnp.mean(jnp.square(h32 - mu), axis=-1, keepdims=True)
    h = ((h32 - mu) * lax.rsqrt(var + CONV_LN_EPS) * ln_g + ln_b).astype(h.dtype)
    return jax.nn.silu(h) @ pw_w + pw_b


def nsa_attention(q, k_cmp, v_cmp, k_slc, v_slc, k_win, v_win, gates, cmp_pe, cmp_w):
    B, G, R, T, d = q.shape
    scale = d ** -0.5
    pos = jnp.arange(T)
    nq = T // Q_BLOCK

    n_cmp = (T - NSA_CMP_LEN) // NSA_CMP_STRIDE + 1
    starts = jnp.arange(n_cmp) * NSA_CMP_STRIDE
    idx = starts[:, None] + jnp.arange(NSA_CMP_LEN)[None, :]

    def compress(t, pe, w):
        blocks = t[:, :, idx] + pe
        return blocks.reshape(B, G, n_cmp, NSA_CMP_LEN * d) @ w

    kc = compress(k_cmp, cmp_pe[0], cmp_w[0])
    vc = compress(v_cmp, cmp_pe[1], cmp_w[1])
    s = jnp.einsum('bgrtd,bgnd->bgrtn', q, kc) * scale
    cmp_mask = (starts + NSA_CMP_LEN - 1)[None, :] <= pos[:, None]
    p_cmp = masked_softmax(s, cmp_mask)
    o_cmp = jnp.einsum('bgrtn,bgnd->bgrtd', p_cmp.astype(vc.dtype), vc)

    n_sblk = T // NSA_SEL_LEN
    n_sel = min(NSA_N_SEL, n_sblk)
    sel_start = jnp.arange(n_sblk) * NSA_SEL_LEN
    overlap = jnp.clip(jnp.minimum(starts[:, None] + NSA_CMP_LEN, sel_start[None, :] + NSA_SEL_LEN)
                       - jnp.maximum(starts[:, None], sel_start[None, :]), 0, None)
    overlap = overlap.astype(jnp.float32) / NSA_CMP_LEN
    imp = jnp.einsum('bgrtn,ns->bgts', p_cmp, overlap)
    cur = pos // NSA_SEL_LEN
    blk = jnp.arange(n_sblk)
    forced = (blk[None, :] == 0) | (blk[None, :] == cur[:, None]) | (blk[None, :] == cur[:, None] - 1)
    eligible = blk[None, :] <= cur[:, None]
    score = jnp.where(eligible, imp + jnp.where(forced, NSA_FORCE_BONUS, 0.0), -jnp.inf)
    _, sel_idx = lax.top_k(score, n_sel)

    kb = k_slc.reshape(B, G, n_sblk, NSA_SEL_LEN, d)
    vb = v_slc.reshape(B, G, n_sblk, NSA_SEL_LEN, d)
    bi = jnp.arange(B)[:, None, None, None]
    gi = jnp.arange(G)[None, :, None, None]
    qb = q.reshape(B, G, R, nq, Q_BLOCK, d).transpose(3, 0, 1, 2, 4, 5)
    ib = sel_idx.reshape(B, G, nq, Q_BLOCK, n_sel).transpose(2, 0, 1, 3, 4)
    tb = pos.reshape(nq, Q_BLOCK)

    def sel_block(args):
        qi, ii, ti = args
        kg = kb[bi, gi, ii].reshape(B, G, Q_BLOCK, n_sel * NSA_SEL_LEN, d)
        vg = vb[bi, gi, ii].reshape(B, G, Q_BLOCK, n_sel * NSA_SEL_LEN, d)
        kpos = (ii[..., None] * NSA_SEL_LEN + jnp.arange(NSA_SEL_LEN)).reshape(B, G, Q_BLOCK, -1)
        s_ = jnp.einsum('bgrqd,bgqkd->bgrqk', qi, kg) * scale
        p = masked_softmax(s_, (kpos <= ti[:, None])[:, :, None])
        return jnp.einsum('bgrqk,bgqkd->bgrqd', p.astype(vg.dtype), vg)

    o_slc = lax.map(sel_block, (qb, ib, tb)).transpose(1, 2, 3, 0, 4, 5).reshape(B, G, R, T, d)

    n_pad = NSA_WINDOW // Q_BLOCK
    n_band = (n_pad + 1) * Q_BLOCK
    bidx = jnp.arange(nq)[:, None] + jnp.arange(n_pad + 1)[None, :]

    def band(t):
        tp = jnp.pad(t, ((0, 0), (0, 0), (NSA_WINDOW, 0), (0, 0))).reshape(B, G, nq + n_pad, Q_BLOCK, d)
        return tp[:, :, bidx].reshape(B, G, nq, n_band, d)

    kw, vw = band(k_win), band(v_win)
    qw = q.reshape(B, G, R, nq, Q_BLOCK, d)
    kpos = jnp.arange(nq)[:, None] * Q_BLOCK - NSA_WINDOW + jnp.arange(n_band)[None, :]
    diff = tb[:, :, None] - kpos[:, None, :]
    wmask = (diff >= 0) & (diff < NSA_WINDOW) & (kpos[:, None, :] >= 0)
    s_w = jnp.einsum('bgrnqd,bgnkd->bgrnqk', qw, kw) * scale
    p_w = masked_softmax(s_w, wmask)
    o_win = jnp.einsum('bgrnqk,bgnkd->bgrnqd', p_w.astype(vw.dtype), vw).reshape(B, G, R, T, d)

    return gates[..., 0:1] * o_cmp + gates[..., 1:2] * o_slc + gates[..., 2:3] * o_win


def mla_attention(u, pos, q_lat_norm, kv_lat_norm, w_uq, w_ukv, q_norm, k_norm):
    B, T, _ = u.shape
    q_lat, kv_lat, k_rope = jnp.split(u, [MLA_Q_RANK, MLA_Q_RANK + MLA_KV_RANK], axis=-1)
    q = (rms_norm(q_lat, q_lat_norm) @ w_uq).reshape(B, T, MLA_HEADS, MLA_QK_DIM)
    kv = (rms_norm(kv_lat, kv_lat_norm) @ w_ukv).reshape(B, T, MLA_HEADS, MLA_NOPE_DIM + MLA_V_DIM)
    k_nope, v = kv[..., :MLA_NOPE_DIM], kv[..., MLA_NOPE_DIM:]
    k = jnp.concatenate([k_nope, jnp.broadcast_to(k_rope[:, :, None, :], (B, T, MLA_HEADS, MLA_ROPE_DIM))], -1)
    q = rms_norm(q, q_norm).transpose(0, 2, 1, 3)
    k = rms_norm(k, k_norm).transpose(0, 2, 1, 3)
    q = jnp.concatenate([q[..., :MLA_NOPE_DIM], rope(q[..., MLA_NOPE_DIM:], pos)], -1)
    k = jnp.concatenate([k[..., :MLA_NOPE_DIM], rope(k[..., MLA_NOPE_DIM:], pos)], -1)
    o = causal_attention(q, k, v.transpose(0, 2, 1, 3))
    return o.transpose(0, 2, 1, 3).reshape(B, T, MLA_HEADS * MLA_V_DIM)


def hybrid_mixer(h, w_in, conv_dw_w, conv_dw_b, conv_ln_g, conv_ln_b, conv_pw_w, conv_pw_b,
                 nsa_q_norm, nsa_k_norm, nsa_cmp_pe, nsa_cmp_w,
                 mla_q_lat_norm, mla_kv_lat_norm, mla_w_uq, mla_w_ukv, mla_q_norm, mla_k_norm,
                 group_norm, w_o):
    B, T, _ = h.shape
    pos = jnp.arange(T)
    u = h @ w_in
    u_sb, u_conv, u_nsa, u_mla = jnp.split(u, [OFF_CONV, OFF_NSA, OFF_MLA], axis=-1)

    def heads(t, n):
        return t.reshape(B, T, n, -1).transpose(0, 2, 1, 3)

    q_sb, k_sb, v_sb = jnp.split(u_sb, 3, axis=-1)
    o_sb = stick_breaking_attention(heads(q_sb, SB_HEADS), heads(k_sb, SB_HEADS), heads(v_sb, SB_HEADS))
    o_sb = o_sb.transpose(0, 2, 1, 3).reshape(B, T, SB_HEADS * HEAD_DIM)

    o_conv = conformer_conv(u_conv, conv_dw_w, conv_dw_b, conv_ln_g, conv_ln_b, conv_pw_w, conv_pw_b)

    q_n = u_nsa[..., :NSA_Q_COLS]
    kv_n = u_nsa[..., NSA_Q_COLS:NSA_Q_COLS + 6 * NSA_KV_COLS]
    g_n = u_nsa[..., NSA_Q_COLS + 6 * NSA_KV_COLS:]
    q_n = rope(rms_norm(q_n.reshape(B, T, NSA_HEADS, HEAD_DIM), nsa_q_norm).transpose(0, 2, 1, 3), pos)
    q_n = q_n.reshape(B, NSA_KV_HEADS, NSA_REP, T, HEAD_DIM)
    kc, vc, ks, vs, kw, vw = [t.reshape(B, T, NSA_KV_HEADS, HEAD_DIM) for t in jnp.split(kv_n, 6, axis=-1)]
    kc, ks, kw = [rope(rms_norm(kk, nsa_k_norm[i]).transpose(0, 2, 1, 3), pos) for i, kk in enumerate((kc, ks, kw))]
    vc, vs, vw = [vv.transpose(0, 2, 1, 3) for vv in (vc, vs, vw)]
    gates = jax.nn.sigmoid(g_n).reshape(B, T, NSA_HEADS, 3).transpose(0, 2, 1, 3)
    gates = gates.reshape(B, NSA_KV_HEADS, NSA_REP, T, 3)
    o_nsa = nsa_attention(q_n, kc, vc, ks, vs, kw, vw, gates, nsa_cmp_pe, nsa_cmp_w)
    o_nsa = o_nsa.reshape(B, NSA_HEADS, T, HEAD_DIM).transpose(0, 2, 1, 3).reshape(B, T, NSA_Q_COLS)

    o_mla = mla_attention(u_mla, pos, mla_q_lat_norm, mla_kv_lat_norm, mla_w_uq, mla_w_ukv, mla_q_norm, mla_k_norm)

    y = jnp.concatenate([o_sb, o_conv, o_nsa, o_mla], axis=-1).reshape(B, T, N_GROUPS, GROUP_WIDTH)
    y = rms_norm(y, group_norm.reshape(N_GROUPS, GROUP_WIDTH)).reshape(B, T, MIX_WIDTH)
    return y @ w_o


def conv_glu_ffn(h, w_up, conv_w, conv_b, w_down):
    u = causal_dwconv(h @ w_up, conv_w, conv_b)
    a, g = jnp.split(u, 2, axis=-1)
    return (jax.nn.silu(g) * a) @ w_down


def setup_inputs(seed: int = 0) -> dict:
    key = jax.random.key(seed)
    ks = list(jax.random.split(key, 40))
    L, D = DEPTH, D_MODEL

    def nrm(shape, scale):
        return jax.random.normal(ks.pop(), shape, jnp.float32) * scale

    def gain(shape):
        return 1.0 + nrm(shape, 0.1)

    return {
        'x': nrm((BATCH, SEQ, D), 1.0),
        'c': nrm((BATCH, D), 1.0),
        'ada_w': nrm((L, D, 6 * D), 0.5 * D ** -0.5),
        'ada_b': nrm((L, 6 * D), 0.02),
        'norm_mix': gain((L, D)),
        'norm_ffn': gain((L, D)),
        'w_in': nrm((L, D, IN_COLS), D ** -0.5),
        'conv_dw_w': nrm((L, CONV_WIDTH, CONV_CH), CONV_WIDTH ** -0.5),
        'conv_dw_b': nrm((L, CONV_CH), 0.02),
        'conv_ln_g': gain((L, CONV_CH)),
        'conv_ln_b': nrm((L, CONV_CH), 0.02),
        'conv_pw_w': nrm((L, CONV_CH, CONV_CH), CONV_CH ** -0.5),
        'conv_pw_b': nrm((L, CONV_CH), 0.02),
        'nsa_q_norm': gain((L, HEAD_DIM)),
        'nsa_k_norm': gain((L, 3, HEAD_DIM)),
        'nsa_cmp_pe': nrm((L, 2, NSA_CMP_LEN, HEAD_DIM), 0.1),
        'nsa_cmp_w': nrm((L, 2, NSA_CMP_LEN * HEAD_DIM, HEAD_DIM), (NSA_CMP_LEN * HEAD_DIM) ** -0.5),
        'mla_q_lat_norm': gain((L, MLA_Q_RANK)),
        'mla_kv_lat_norm': gain((L, MLA_KV_RANK)),
        'mla_w_uq': nrm((L, MLA_Q_RANK, MLA_HEADS * MLA_QK_DIM), MLA_Q_RANK ** -0.5),
        'mla_w_ukv': nrm((L, MLA_KV_RANK, MLA_HEADS * (MLA_NOPE_DIM + MLA_V_DIM)), MLA_KV_RANK ** -0.5),
        'mla_q_norm': gain((L, MLA_QK_DIM)),
        'mla_k_norm': gain((L, MLA_QK_DIM)),
        'group_norm': gain((L, MIX_WIDTH)),
        'w_o': nrm((L, MIX_WIDTH, D), MIX_WIDTH ** -0.5),
        'ffn_up': nrm((L, D, 2 * D_FF), D ** -0.5),
        'ffn_conv_w': nrm((L, FFN_CONV_WIDTH, 2 * D_FF), FFN_CONV_WIDTH ** -0.5),
        'ffn_conv_b': nrm((L, 2 * D_FF), 0.02),
        'ffn_down': nrm((L, D_FF, D), D_FF ** -0.5),
    }


def reference(x, c, ada_w, ada_b, norm_mix, norm_ffn, w_in, conv_dw_w, conv_dw_b, conv_ln_g, conv_ln_b,
              conv_pw_w, conv_pw_b, nsa_q_norm, nsa_k_norm, nsa_cmp_pe, nsa_cmp_w,
              mla_q_lat_norm, mla_kv_lat_norm, mla_w_uq, mla_w_ukv, mla_q_norm, mla_k_norm,
              group_norm, w_o, ffn_up, ffn_conv_w, ffn_conv_b, ffn_down):
    cond = jax.nn.silu(c)
    for l in range(DEPTH):
        mod = cond @ ada_w[l] + ada_b[l]
        sh1, sc1, g1, sh2, sc2, g2 = [m[:, None, :] for m in jnp.split(mod, 6, axis=-1)]
        h = rms_norm(x, norm_mix[l]) * (1.0 + sc1) + sh1
        x = x + g1 * hybrid_mixer(h, w_in[l], conv_dw_w[l], conv_dw_b[l], conv_ln_g[l], conv_ln_b[l],
                                  conv_pw_w[l], conv_pw_b[l], nsa_q_norm[l], nsa_k_norm[l],
                                  nsa_cmp_pe[l], nsa_cmp_w[l], mla_q_lat_norm[l], mla_kv_lat_norm[l],
                                  mla_w_uq[l], mla_w_ukv[l], mla_q_norm[l], mla_k_norm[l],
                                  group_norm[l], w_o[l])
        h = rms_norm(x, norm_ffn[l]) * (1.0 + sc2) + sh2
        x = x + g2 * conv_glu_ffn(h, ffn_up[l], ffn_conv_w[l], ffn_conv_b[l], ffn_down[l])
    return x
```

```python
import contextlib
import numpy as np
import concourse.bass as bass
import concourse.mybir as mybir
from concourse.bass_utils import run_bass_kernel_spmd

F32 = mybir.dt.float32
BF16 = mybir.dt.bfloat16
AF = mybir.ActivationFunctionType
ALU = mybir.AluOpType
AX = mybir.AxisListType

ENGS = ("tensor", "vector", "scalar", "gpsimd", "sync")
N_DMA_SEM = 16

D = 2048
NFC = 16
DFF = 5632
NCC = 44
SEG = 512
NSEG = 2
NT = SEG * NSEG
EPS = 1e-6


class Buf:
    __slots__ = ("name", "w", "r", "excl")

    def __init__(self, name="", excl=False):
        self.name = name
        self.w = None
        self.r = {}
        self.excl = excl


class Sched:
    def __init__(self, nc):
        self.nc = nc
        self.q = {e: [] for e in ENGS}
        self.cnt = {}
        self.waited = {e: {} for e in ENGS}
        self.dma_cnt = {}
        self.sems = {}
        self.n_inst = 0

    def _need(self, eng, tok):
        if tok is None:
            return
        key, val = tok
        if key == "tensor" and eng == "tensor":
            return
        w = self.waited[eng]
        if w.get(key, 0) >= val:
            return
        w[key] = val
        self.q[eng].append(("wait", key, val))

    def _deps(self, eng, reads, writes):
        for b in reads:
            self._need(eng, b.w)
            if b.excl:
                for k, v in b.r.items():
                    if k != eng:
                        self._need(eng, (k, v))
        for b in writes:
            self._need(eng, b.w)
            for k, v in b.r.items():
                self._need(eng, (k, v))

    def _commit(self, tok, reads, writes):
        k, v = tok
        for b in reads:
            if b.r.get(k, 0) < v:
                b.r[k] = v
        for b in writes:
            b.w = tok
            b.r = {}

    def op(self, eng, fn, reads=(), writes=()):
        self._deps(eng, reads, writes)
        v = self.cnt.get(eng, 0) + 1
        self.cnt[eng] = v
        self.q[eng].append(("inst", fn, eng, 1))
        self._commit((eng, v), reads, writes)
        self.n_inst += 1

    def dma(self, eng, fn, reads=(), writes=()):
        self._deps(eng, reads, writes)
        di = self.dma_cnt.get(eng, 0)
        self.dma_cnt[eng] = di + 1
        key = "dma_%s_%d" % (eng, di % N_DMA_SEM)
        prev = self.cnt.get(key, 0)
        if prev:
            self._need(eng, (key, prev))
        v = prev + 16
        self.cnt[key] = v
        self.q[eng].append(("inst", fn, key, 16))
        self._commit((key, v), reads, writes)
        self.n_inst += 1

    def barrier(self):
        for eng in ENGS:
            for key, val in list(self.cnt.items()):
                self._need(eng, (key, val))

    def wait_all(self, eng, bufs):
        for b in bufs:
            self._need(eng, b.w)

    def emit(self):
        nc = self.nc
        keys = list(ENGS) + [k for k in self.cnt if k.startswith("dma_")]
        with contextlib.ExitStack() as st:
            for k in keys:
                self.sems[k] = st.enter_context(nc.semaphore("s_" + k))
            block = st.enter_context(nc.Block())
            sems = self.sems

            def run(eng_name):
                def body(e):
                    for it in self.q[eng_name]:
                        if it[0] == "wait":
                            e.wait_ge(sems[it[1]], it[2])
                        else:
                            it[1](e).then_inc(sems[it[2]], it[3])
                return body
            block.tensor(run("tensor"))
            block.vector(run("vector"))
            block.scalar(run("scalar"))
            block.gpsimd(run("gpsimd"))
            block.sync(run("sync"))


class Ring:
    def __init__(self, K, name, shape, dt, n, psum=False):
        self.tiles = []
        for i in range(n):
            t = K.psum(name + str(i), shape, dt) if psum else K.sbuf(name + str(i), shape, dt)
            self.tiles.append((t, Buf(name + str(i), excl=psum)))
        self.i = 0

    def next(self):
        t = self.tiles[self.i % len(self.tiles)]
        self.i += 1
        return t


class WStream:
    def __init__(self, K, ring, srcs, shape_fn=None):
        self.K, self.ring, self.srcs = K, ring, srcs
        self.issued = 0
        self.n = len(ring.tiles)

    def _issue(self):
        j = self.issued
        t, B = self.ring.tiles[j % self.n]
        self.K.load(self.dst(t), self.srcs[j], [B], eng="gpsimd")
        self.issued += 1

    def dst(self, t):
        return t[:, :, :]

    def prefetch(self, upto):
        while self.issued < min(len(self.srcs), upto):
            self._issue()

    def get(self, i):
        self.prefetch(i + self.n)
        return self.ring.tiles[i % self.n]


class Kern:
    def __init__(self):
        self.nc = bass.Bass("TRN2", target_bir_lowering=False)
        self.S = Sched(self.nc)
        self.st = contextlib.ExitStack()
        self.dma_rr = 0
        self.eps_t = self.sbuf("eps_t", [128, 4], F32)
        self.epsB = Buf("eps")
        self.S.op("vector", lambda e: e.memset(self.eps_t[:, 0:1], EPS), writes=[self.epsB])
        self.S.op("vector", lambda e: e.memset(self.eps_t[:, 1:2], 1e-5), writes=[self.epsB])
        self.S.op("vector", lambda e: e.memset(self.eps_t[:, 2:3], 1.0), writes=[self.epsB])
        self.S.op("vector", lambda e: e.memset(self.eps_t[:, 3:4], 0.0), writes=[self.epsB])

    @contextlib.contextmanager
    def scope(self):
        outer = self.st
        self.st = contextlib.ExitStack()
        try:
            yield
        finally:
            self.S.barrier()
            self.st.close()
            self.st = outer

    def sbuf(self, name, shape, dt):
        return self.st.enter_context(self.nc.sbuf_tensor("sb_" + name, list(shape), dt))

    def psum(self, name, shape, dt):
        return self.st.enter_context(self.nc.psum_tensor("pp_" + name, list(shape), dt))

    def dram_in(self, name, shape, dt=F32):
        return self.nc.dram_tensor(name, list(shape), dt, kind="ExternalInput").ap()

    def dram_out(self, name, shape, dt=F32):
        return self.nc.dram_tensor(name, list(shape), dt, kind="ExternalOutput").ap()

    def load(self, out_ap, in_ap, wbufs, rbufs=(), eng=None):
        if eng is None:
            eng = "sync"
        self.S.dma(eng, lambda e: e.dma_start(out=out_ap, in_=in_ap), reads=rbufs, writes=wbufs)

    def finish(self, out_bufs):
        self.S.wait_all("sync", out_bufs)
        self.S.emit()
        self.st.close()
        return self.nc


def rms_mod_to_bf16(K, xT, xB, gm, sh, hT, hB, ones_bf, ps_ring, tmp_ring, sq_ring, rstd_t, extra=(), nseg=NSEG, seg=SEG):
    S = K.S
    for s in range(nseg):
        ps, psB = ps_ring.next()
        for fc in range(NFC):
            sq, sqB = sq_ring.next()
            S.op("scalar", lambda e, sq=sq, fc=fc, s=s: e.activation(out=sq[:, :seg], in_=xT[:, fc, s * seg:(s + 1) * seg], func=AF.Square),
                 reads=[xB[fc][s]], writes=[sqB])
            S.op("tensor", lambda e, sq=sq, ps=ps, fc=fc: e.matmul(ps[:, :seg], ones_bf[:, :], sq[:, :seg], start=(fc == 0), stop=(fc == NFC - 1)),
                 reads=[sqB], writes=[psB])
        rs = rstd_t[:, s * seg:(s + 1) * seg]
        rsB = Buf("rstd")
        S.op("scalar", lambda e, ps=ps, rs=rs: e.activation(out=rs, in_=ps[:, :seg], func=AF.Sqrt, scale=1.0 / D, bias=K.eps_t[:, 0:1]),
             reads=[psB, K.epsB], writes=[rsB])
        S.op("vector", lambda e, rs=rs: e.reciprocal(out=rs, in_=rs),
             reads=[rsB], writes=[rsB])
        for fc in range(NFC):
            tmp, tmpB = tmp_ring.next()
            S.op("vector", lambda e, tmp=tmp, fc=fc, s=s, rs=rs: e.scalar_tensor_tensor(
                out=tmp[:, :seg], in0=xT[:, fc, s * seg:(s + 1) * seg], scalar=gm[:, fc:fc + 1], in1=rs, op0=ALU.mult, op1=ALU.mult),
                reads=[xB[fc][s], rsB] + list(extra), writes=[tmpB])
            S.op("scalar", lambda e, tmp=tmp, fc=fc, s=s: e.activation(out=hT[:, fc, s, :], in_=tmp[:, :seg], func=AF.Identity, bias=sh[:, fc:fc + 1], scale=1.0),
                 reads=[tmpB] + list(extra), writes=[hB[fc][s]])


def ffn_phase(K, xT, xB, xh, xhB, hv, modT, gain, w_up, w_dn, cw_d, cb_d, P):
    S = K.S
    sh2 = modT[:, 48:64]
    sc2 = modT[:, 64:80]
    g2 = modT[:, 80:96]
    constB = P["constB"]
    gm = K.sbuf("ffn_gm", [128, NFC], F32)
    gmB = Buf("gm")
    S.op("vector", lambda e: e.scalar_tensor_tensor(out=gm[:, :], in0=sc2, scalar=1.0, in1=gain[:, :], op0=ALU.add, op1=ALU.mult),
         reads=[constB], writes=[gmB])
    cw = K.sbuf("ffn_cw", [128, 2 * NCC, 3], F32)
    cb = K.sbuf("ffn_cb", [128, 2 * NCC], F32)
    cwB = Buf("cw")
    K.load(cw[:, :, :], cw_d, [cwB])
    K.load(cb[:, :], cb_d, [cwB])

    hT = K.sbuf("ffn_hT", [128, NFC, NSEG, SEG], BF16)
    hB = [[Buf("h%d_%d" % (fc, s)) for s in range(NSEG)] for fc in range(NFC)]
    rstd_t = K.sbuf("ffn_rstd", [128, NT], F32)
    rms_mod_to_bf16(K, xT, xB, gm, sh2, hT, hB, P["ones_bf"], P["ps_up"], P["acc"], P["sq"], rstd_t, extra=[gmB, constB])
    hh = K.sbuf("ffn_hh", [128, NFC, 16], BF16)
    hhB = Buf("hh")
    sqh = K.sbuf("ffn_sqh", [128, NFC * 16], BF16)
    sqhB = Buf("sqh")
    tmph = K.sbuf("ffn_tmph", [128, NFC, 16], F32)
    tmphB = Buf("tmph")
    rsh = K.sbuf("ffn_rsh", [128, 16], F32)
    rshB = Buf("rsh")
    S.op("scalar", lambda e: e.activation(out=sqh[:, :], in_=xh.rearrange("p a b -> p (a b)"), func=AF.Square), reads=[xhB], writes=[sqhB])
    psh, pshB = P["ps_h"].next()
    for fc in range(NFC):
        S.op("tensor", lambda e, fc=fc: e.matmul(psh[:, 0:16], P["ones_bf"][:, :], sqh[:, fc * 16:(fc + 1) * 16], start=(fc == 0), stop=(fc == NFC - 1)),
             reads=[sqhB], writes=[pshB])
    S.op("scalar", lambda e: e.activation(out=rsh[:, :], in_=psh[:, 0:16], func=AF.Sqrt, scale=1.0 / D, bias=K.eps_t[:, 0:1]), reads=[pshB, K.epsB], writes=[rshB])
    S.op("vector", lambda e: e.reciprocal(out=rsh[:, :], in_=rsh[:, :]), reads=[rshB], writes=[rshB])
    for fc in range(NFC):
        S.op("vector", lambda e, fc=fc: e.scalar_tensor_tensor(out=tmph[:, fc, :], in0=xh[:, fc, :], scalar=gm[:, fc:fc + 1], in1=rsh[:, :], op0=ALU.mult, op1=ALU.mult),
             reads=[xhB, rshB, gmB], writes=[tmphB])
        S.op("vector", lambda e, fc=fc: e.scalar_tensor_tensor(out=hh[:, fc, :], in0=tmph[:, fc, :], scalar=sh2[:, fc:fc + 1], in1=hv[:, fc * 16:(fc + 1) * 16], op0=ALU.add, op1=ALU.mult),
             reads=[tmphB, constB], writes=[hhB])

    GK = 11
    zT = K.sbuf("ffn_zT", [128, GK, NSEG, SEG], BF16)
    zB = [[Buf("z%d_%d" % (k, s)) for s in range(NSEG)] for k in range(GK)]
    up_order = []
    for g in range(NCC // GK):
        for kl in range(GK):
            up_order += [g * GK + kl, NCC + g * GK + kl]
    wup_s = WStream(K, P["wup"], [w_up[cc].rearrange("p (k j) -> p k j", j=128) for cc in up_order])
    wdn_s = WStream(K, P["wdn"], [w_dn[g, fc].rearrange("p (k j) -> p k j", j=128) for g in range(NCC // GK) for fc in range(NFC)])
    up_i = 0
    dn_i = 0
    ps_up = P["ps_up"]
    ps_h = P["ps_h"]
    ps_dn = P["ps_dn"]
    acc_ring = P["acc"]
    for g in range(NCC // GK):
        for kl in range(GK):
            cca = g * GK + kl
            accs = {}
            for br, cc in (("a", cca), ("g", NCC + cca)):
                wt, wB = wup_s.get(up_i)
                up_i += 1
                pss = [ps_up.next() for _ in range(NSEG)]
                ph, phB = ps_h.next()
                for kc in range(NFC):
                    for s in range(NSEG):
                        S.op("tensor", lambda e, wt=wt, kc=kc, s=s, p=pss[s][0]: e.matmul(p[:, :], wt[:, kc, :], hT[:, kc, s, :], start=(kc == 0), stop=(kc == NFC - 1)),
                             reads=[wB, hB[kc][s]], writes=[pss[s][1]])
                    S.op("tensor", lambda e, wt=wt, kc=kc, ph=ph: e.matmul(ph[:, 0:16], wt[:, kc, :], hh[:, kc, :], start=(kc == 0), stop=(kc == NFC - 1)),
                         reads=[wB, hhB], writes=[phB])
                for s in range(NSEG):
                    p, pB = pss[s]
                    a, aB = acc_ring.next()
                    accs[(br, s)] = (a, aB)
                    S.op("scalar", lambda e, a=a, p=p, cc=cc: e.activation(out=a[:, :], in_=p[:, :], func=AF.Identity, scale=cw[:, cc, 2:3], bias=cb[:, cc:cc + 1]),
                         reads=[pB, cwB], writes=[aB])
                    a3 = a[:, :].rearrange("p (t n) -> p t n", n=128)
                    p3 = p[:, :].rearrange("p (t n) -> p t n", n=128)
                    h3 = ph[:, 0:16].rearrange("p (t n) -> p t n", n=2)[:, 4 * s:4 * s + 4, :]
                    S.op("vector", lambda e, a3=a3, p3=p3, cc=cc: e.scalar_tensor_tensor(out=a3[:, :, 1:128], in0=p3[:, :, 0:127], scalar=cw[:, cc, 1:2], in1=a3[:, :, 1:128], op0=ALU.mult, op1=ALU.add),
                         reads=[pB, cwB], writes=[aB])
                    S.op("vector", lambda e, a3=a3, p3=p3, cc=cc: e.scalar_tensor_tensor(out=a3[:, :, 2:128], in0=p3[:, :, 0:126], scalar=cw[:, cc, 0:1], in1=a3[:, :, 2:128], op0=ALU.mult, op1=ALU.add),
                         reads=[pB, cwB], writes=[aB])
                    S.op("vector", lambda e, a3=a3, h3=h3, cc=cc: e.scalar_tensor_tensor(out=a3[:, :, 0:2], in0=h3, scalar=cw[:, cc, 0:1], in1=a3[:, :, 0:2], op0=ALU.mult, op1=ALU.add),
                         reads=[phB, cwB], writes=[aB])
                    S.op("vector", lambda e, a3=a3, h3=h3, cc=cc: e.scalar_tensor_tensor(out=a3[:, :, 0:1], in0=h3[:, :, 1:2], scalar=cw[:, cc, 1:2], in1=a3[:, :, 0:1], op0=ALU.mult, op1=ALU.add),
                         reads=[phB, cwB], writes=[aB])
            for s in range(NSEG):
                aa, aaB = accs[("a", s)]
                ag, agB = accs[("g", s)]
                S.op("scalar", lambda e, ag=ag: e.activation(out=ag[:, :], in_=ag[:, :], func=AF.Silu), reads=[agB], writes=[agB])
                S.op("vector", lambda e, aa=aa, ag=ag, kl=kl, s=s: e.tensor_tensor(out=zT[:, kl, s, :], in0=ag[:, :], in1=aa[:, :], op=ALU.mult),
                     reads=[aaB, agB], writes=[zB[kl][s]])
        for fc in range(NFC):
            wt, wB = wdn_s.get(dn_i)
            dn_i += 1
            for s in range(NSEG):
                p, pB = ps_dn.next()
                for kl in range(GK):
                    S.op("tensor", lambda e, wt=wt, kl=kl, s=s, p=p: e.matmul(p[:, :], wt[:, kl, :], zT[:, kl, s, :], start=(kl == 0), stop=(kl == GK - 1)),
                         reads=[wB, zB[kl][s]], writes=[pB])
                S.op("vector", lambda e, p=p, fc=fc, s=s: e.scalar_tensor_tensor(out=xT[:, fc, s * SEG:(s + 1) * SEG], in0=p[:, :], scalar=g2[:, fc:fc + 1],
                                                                                   in1=xT[:, fc, s * SEG:(s + 1) * SEG], op0=ALU.mult, op1=ALU.add),
                     reads=[pB, constB], writes=[xB[fc][s]])


def make_pools(K, phase="all"):
    P = {}
    P["constB"] = Buf("const")
    ones = K.sbuf("ones_bf", [128, 128], BF16)
    K.S.op("vector", lambda e: e.memset(ones[:, :], 1.0), writes=[P["constB"]])
    P["ones_bf"] = ones
    P["ps_up"] = Ring(K, "ps_up", [128, 512], F32, 4, psum=True)
    P["ps_dn"] = Ring(K, "ps_dn", [128, 512], F32, 3, psum=True)
    P["ps_o"] = P["ps_dn"]
    misc = K.psum("ps_misc", [128, 512], F32)
    P["ps_h"] = Ring.__new__(Ring)
    miscB = Buf("ps_misc", excl=True)
    P["ps_h"].tiles, P["ps_h"].i = [(misc[:, 0:16], miscB)], 0
    P["ps_tr"] = Ring.__new__(Ring)
    P["ps_tr"].tiles, P["ps_tr"].i = [(misc[:, 256:512].bitcast(BF16), miscB)], 0
    P["acc"] = Ring(K, "acc", [128, 512], F32, 6)
    P["sq"] = Ring(K, "sq", [128, 512], BF16, 3)
    P["stg"] = Ring(K, "stg", [128, 512], BF16, 4)
    P["pt"] = Ring(K, "pt", [128, 512], BF16, 4)
    P["small"] = Ring(K, "small", [128, 16], F32, 8)
    P["wup"] = Ring(K, "wup", [128, NFC, 128], BF16, 4)
    if phase in ("all", "ffn"):
        P["wdn"] = Ring(K, "wdn", [128, 11, 128], BF16, 4)
    return P


NBF = 2592
NF32 = 512


def load_consts(K, P, cbf_d, cebig_d, cf32_d, rope_d=None):
    C = {}
    cB = P["constB"]
    cbf = K.sbuf("cbf", [128, NBF], BF16)
    K.load(cbf[:, :], cbf_d, [cB], eng="gpsimd")
    ceb = K.sbuf("cebig", [32, 2048], BF16)
    K.load(ceb[:, :], cebig_d, [cB], eng="gpsimd")
    cf = K.sbuf("cf32", [128, NF32], F32)
    K.load(cf[:, :], cf32_d, [cB])
    C["ident"] = cbf[:, 0:128]
    id32 = K.sbuf("ident32", [128, 128], F32)
    K.S.op("vector", lambda e: e.tensor_copy(out=id32[:, :], in_=cbf[:, 0:128]), reads=[cB], writes=[cB])
    C["ident32"] = id32[:, :]
    C["negtri"] = cbf[:, 128:256]
    C["mA"] = cbf[:, 256:384]
    C["mB"] = cbf[:, 384:512]
    C["sbA"] = cbf[:, 512:640]
    C["sbB"] = cbf[:, 640:768]
    C["mW"] = cbf[:, 768:1536]
    C["cmpmask"] = cbf[:, 1536:2560]
    C["ov"] = cbf[:, 2560:2592]
    C["ebig"] = ceb
    C["R64"] = cf[0:64, 0:64]
    C["R96"] = cf[0:96, 64:160]
    C["sel"] = cf[0:32, 160:256]
    C["tkbias"] = cf[:, 256:512].rearrange("p (i s) -> p i s", s=32)
    if rope_d is not None:
        rp = K.sbuf("rope", [128, 4, NT], F32)
        K.load(rp[:, :, :], rope_d, [cB])
        C["cosn"], C["sinn"], C["cosm"], C["sinm"] = rp[:, 0, :], rp[:, 1, :], rp[:, 2, :], rp[:, 3, :]
    return C


PK_OWN = {"sb_q": ([4, 128, NT], BF16), "nsa_q": ([64, 8, NT], BF16), "mla_q": ([8, 96, NT], BF16), "gates": ([NT, 24], F32), "cv": ([4, 128, NT], BF16)}
PK_KV = {"sb_k": ([4, 128, NT], BF16), "sb_v": ([NT, 512], BF16), "nsa_k": ([3, 64, 2, NT], BF16), "nsa_vc": ([64, 2, NT], BF16), "nsa_v": ([NT, 256], BF16),
         "mla_k": ([8, 96, NT], BF16), "mla_v": ([NT, 512], BF16)}


def full_shape(shape, name):
    return [2 * NT if d == NT else d for d in shape]


def load_x(K, x_d):
    xT = K.sbuf("xT", [128, NFC, NT], F32)
    xB = [[Buf("x%d_%d" % (fc, s)) for s in range(NSEG)] for fc in range(NFC)]
    for fc in range(NFC):
        for s in range(NSEG):
            K.load(xT[:, fc, s * SEG:(s + 1) * SEG], x_d[:, fc, s * SEG:(s + 1) * SEG], [xB[fc][s]])
    return xT, xB


def store_x(K, xT, xB, y_d):
    outs = []
    for fc in range(NFC):
        for s in range(NSEG):
            oB = Buf("out")
            outs.append(oB)
            K.S.dma("sync", lambda e, fc=fc, s=s: e.dma_start(out=y_d[:, fc, s * SEG:(s + 1) * SEG], in_=xT[:, fc, s * SEG:(s + 1) * SEG]),
                    reads=[xB[fc][s]], writes=[oB])
    return outs


def build_launch_a():
    K = Kern()
    x_d = K.dram_in("xT", [128, NFC, NT])
    mod_d = K.dram_in("modT", [128, 96])
    gain_d = K.dram_in("gain", [128, NFC])
    cbf_d = K.dram_in("cbf", [128, NBF])
    ceb_d = K.dram_in("cebig", [32, 2048])
    cf_d = K.dram_in("cf32", [128, NF32])
    rope_d = K.dram_in("rope", [128, 4, NT])
    W = {"w_fm": K.dram_in("w_fm", [NFM, 128, D]), "w_tm": K.dram_in("w_tm", [128, NFC * TMW]), "svec": K.dram_in("svec", [128, 16]),
         "w_uq": K.dram_in("w_uq", [128, 3 * 768]), "w_uk": K.dram_in("w_uk", [128, 2 * 8 * 96]), "w_uv": K.dram_in("w_uv", [128, 2 * 512])}
    pk = {}
    for n, (shp, dt) in list(PK_OWN.items()) + list(PK_KV.items()):
        pk[n] = K.dram_out("o_" + n, shp, dt)
    P = make_pools(K, "a")
    C = load_consts(K, P, cbf_d, ceb_d, cf_d, rope_d)
    xT, xB = load_x(K, x_d)
    modT = K.sbuf("modT", [128, 96], F32)
    gain = K.sbuf("gain", [128, NFC], F32)
    K.load(modT[:, :], mod_d, [P["constB"]])
    K.load(gain[:, :], gain_d, [P["constB"]])
    outs = phase_a(K, P, xT, xB, modT, gain, C, W, pk)
    return K.finish(outs)


def build_launch_b():
    K = Kern()
    x_d = K.dram_in("xT", [128, NFC, NT])
    mod_d = K.dram_in("modT", [128, 96])
    gn_d = K.dram_in("gn", [128, NFC])
    cbf_d = K.dram_in("cbf", [128, NBF])
    ceb_d = K.dram_in("cebig", [32, 2048])
    cf_d = K.dram_in("cf32", [128, NF32])
    cvh_d = K.dram_in("cvh", [4, 128, NQT, 30], BF16)
    W = {"w_o": K.dram_in("w_o", [NFC, 128, D]), "conv_par": K.dram_in("conv_par", [128, 4, 36]), "conv_pw": K.dram_in("conv_pw", [128, 4 * 512]),
         "conv_pwb": K.dram_in("conv_pwb", [128, 512]), "cmp_wk": K.dram_in("cmp_wk", [128, 32 * 128]), "cmp_wv": K.dram_in("cmp_wv", [128, 32 * 64]),
         "cmp_peT": K.dram_in("cmp_peT", [128, 64])}
    own = {n: K.dram_in("i_" + n, shp, dt) for n, (shp, dt) in PK_OWN.items()}
    kv = {n: K.dram_in("f_" + n, full_shape(shp, n), dt) for n, (shp, dt) in PK_KV.items()}
    y_d = K.dram_out("yT", [128, NFC, NT])
    yn_d = K.dram_out("ynT", [128, NFC, NT], BF16)
    P = make_pools(K, "b")
    C = load_consts(K, P, cbf_d, ceb_d, cf_d)
    modT = K.sbuf("modT", [128, 96], F32)
    gn = K.sbuf("gn", [128, NFC], F32)
    K.load(modT[:, :], mod_d, [P["constB"]])
    K.load(gn[:, :], gn_d, [P["constB"]])
    ynT = K.sbuf("b_ynT", [128, NFC, NT], BF16)
    ynB = [[Buf("yn%d_%d" % (fc, i)) for i in range(NQT)] for fc in range(NFC)]
    phase_b_mixers(K, P, C, W, own, kv, cvh_d, gn, ynT, ynB)
    outs = []
    for fc in range(NFC):
        oB = Buf("o")
        outs.append(oB)
        K.S.dma("sync", lambda e, fc=fc: e.dma_start(out=yn_d[:, fc, :], in_=ynT[:, fc, :]), reads=[ynB[fc][i] for i in range(NQT)], writes=[oB])
    xT, xB = load_x(K, x_d)
    import os
    if not os.environ.get("NOOUT"):
        out_proj(K, P, W, ynT, ynB, xT, xB, modT[:, 32:48])
    outs += store_x(K, xT, xB, y_d)
    return K.finish(outs)


def build_launch_c():
    K = Kern()
    x_d = K.dram_in("xT", [128, NFC, NT])
    xh_d = K.dram_in("xh", [128, NFC, 16])
    hv_d = K.dram_in("hv", [128, NFC * 16])
    mod_d = K.dram_in("modT", [128, 96])
    gain_d = K.dram_in("gain", [128, NFC])
    wup_d = K.dram_in("w_up", [2 * NCC, 128, D])
    wdn_d = K.dram_in("w_dn", [4, NFC, 128, 11 * 128])
    cw_d = K.dram_in("cw", [128, 2 * NCC, 3])
    cb_d = K.dram_in("cb", [128, 2 * NCC])
    y_d = K.dram_out("yT", [128, NFC, NT])
    P = make_pools(K, "ffn")
    xT, xB = load_x(K, x_d)
    xh = K.sbuf("xh", [128, NFC, 16], F32)
    xhB = Buf("xh")
    K.load(xh[:, :, :], xh_d, [xhB])
    hv = K.sbuf("hv", [128, NFC * 16], F32)
    modT = K.sbuf("modT", [128, 96], F32)
    gain = K.sbuf("gain", [128, NFC], F32)
    K.load(hv[:, :], hv_d, [P["constB"]])
    K.load(modT[:, :], mod_d, [P["constB"]])
    K.load(gain[:, :], gain_d, [P["constB"]])
    ffn_phase(K, xT, xB, xh, xhB, hv, modT, gain, wup_d, wdn_d, cw_d, cb_d, P)
    return K.finish(store_x(K, xT, xB, y_d))


NMC = 12


def build_launch_mod():
    K = Kern()
    c_d = K.dram_in("cT", [128, NFC, 4])
    w_d = K.dram_in("ada_w", [2 * NMC, 128, D])
    b_d = K.dram_in("ada_b", [128, 2 * NMC])
    o_d = K.dram_out("modp", [128, 2 * NMC, 4])
    S = K.S
    ps_ring = Ring(K, "ps", [128, 512], F32, 2, psum=True)
    wr = Ring(K, "w", [128, NFC, 128], F32, 3)
    cT = K.sbuf("cT", [128, NFC, 4], F32)
    cB = Buf("c")
    K.load(cT[:, :, :], c_d, [cB])
    bt = K.sbuf("bt", [128, 2 * NMC], F32)
    K.load(bt[:, :], b_d, [cB])
    S.op("scalar", lambda e: e.activation(out=cT[:, :, :], in_=cT[:, :, :], func=AF.Silu), reads=[cB], writes=[cB])
    ot = K.sbuf("ot", [128, 2 * NMC, 4], F32)
    oB = Buf("ot")
    for ch in range(2 * NMC):
        wt, wB = wr.next()
        K.load(wt[:, :, :], w_d[ch].rearrange("p (k j) -> p k j", j=128), [wB])
        ps, psB = ps_ring.next()
        for kc in range(NFC):
            S.op("tensor", lambda e, wt=wt, ps=ps, kc=kc: e.matmul(ps[:, 0:4], wt[:, kc, :], cT[:, kc, :], start=(kc == 0), stop=(kc == NFC - 1)), reads=[wB, cB], writes=[psB])
        S.op("scalar", lambda e, ps=ps, ch=ch: e.activation(out=ot[:, ch, :], in_=ps[:, 0:4], func=AF.Identity, bias=bt[:, ch:ch + 1], scale=1.0), reads=[psB, cB], writes=[oB])
    fin = Buf("fin")
    S.dma("sync", lambda e: e.dma_start(out=o_d, in_=ot[:, :, :]), reads=[oB], writes=[fin])
    return K.finish([fin])


HD = 64
OFF_CONV, OFF_NSA, OFF_MLA = 1536, 2560, 3864
NFM = 38
TMW = 792
NEG = -30000.0


def fm_chunk_cols():
    ch = []
    for i in range(4):
        ch.append(list(range(i * 128, (i + 1) * 128)))
    for i in range(4):
        ch.append(list(range(512 + i * 128, 512 + (i + 1) * 128)))
    for i in range(4):
        ch.append(list(range(OFF_CONV + i * 128, OFF_CONV + (i + 1) * 128)))
        ch.append(list(range(OFF_CONV + 512 + i * 128, OFF_CONV + 512 + (i + 1) * 128)))
    for h in range(8):
        ch.append(list(range(OFF_NSA + h * 64, OFF_NSA + (h + 1) * 64)))
    kvb = OFF_NSA + 512
    for t in (0, 2, 4, 1):
        for g in range(2):
            ch.append(list(range(kvb + t * 128 + g * 64, kvb + t * 128 + (g + 1) * 64)))
    for i in range(3):
        ch.append(list(range(OFF_MLA + i * 128, OFF_MLA + (i + 1) * 128)))
    for i in range(2):
        ch.append(list(range(OFF_MLA + 384 + i * 128, OFF_MLA + 384 + (i + 1) * 128)))
    ch.append(list(range(OFF_MLA + 640, OFF_MLA + 672)))
    return ch


def tm_cols():
    kvb = OFF_NSA + 512
    return (list(range(1024, 1536)) + list(range(kvb + 3 * 128, kvb + 4 * 128)) + list(range(kvb + 5 * 128, kvb + 6 * 128))
            + list(range(OFF_NSA + 512 + 768, OFF_NSA + 512 + 768 + 24)))


def head_norm_rope(K, P, src_ps, srcB, npart, gain_ap, gainB, inv_dim, Rm, cosT, sinT, out_ap, outB, consts):
    S = K.S
    sq, sqB = P["sq"].next()
    S.op("scalar", lambda e: e.activation(out=sq[0:npart, :], in_=src_ps, func=AF.Square), reads=[srcB], writes=[sqB])
    ss, ssB = P["ps_dn"].next()
    S.op("tensor", lambda e: e.matmul(ss[0:npart, :], P["ones_bf"][0:npart, 0:npart], sq[0:npart, :], start=True, stop=True), reads=[sqB, P["constB"]], writes=[ssB])
    rs, rsB = P["acc"].next()
    S.op("scalar", lambda e: e.activation(out=rs[0:npart, :], in_=ss[0:npart, :], func=AF.Sqrt, scale=inv_dim, bias=K.eps_t[0:npart, 0:1]), reads=[ssB, K.epsB], writes=[rsB])
    S.op("vector", lambda e: e.reciprocal(out=rs[0:npart, :], in_=rs[0:npart, :]), reads=[rsB], writes=[rsB])
    xg, xgB = P["acc"].next()
    S.op("vector", lambda e: e.scalar_tensor_tensor(out=xg[0:npart, :], in0=src_ps, scalar=gain_ap, in1=rs[0:npart, :], op0=ALU.mult, op1=ALU.mult),
         reads=[srcB, rsB, gainB], writes=[xgB])
    rot, rotB = P["ps_dn"].next()
    S.op("tensor", lambda e: e.matmul(rot[0:npart, :], Rm, xg[0:npart, :], start=True, stop=True), reads=[xgB, consts], writes=[rotB])
    t1, t1B = P["acc"].next()
    S.op("vector", lambda e: e.tensor_tensor(out=t1[0:npart, :], in0=xg[0:npart, :], in1=cosT, op=ALU.mult), reads=[xgB, consts], writes=[t1B])
    S.op("vector", lambda e: e.tensor_tensor(out=xg[0:npart, :], in0=rot[0:npart, :], in1=sinT, op=ALU.mult), reads=[rotB, consts, xgB], writes=[xgB])
    S.op("vector", lambda e: e.tensor_tensor(out=out_ap, in0=t1[0:npart, :], in1=xg[0:npart, :], op=ALU.add), reads=[t1B, xgB], writes=[outB])


def phase_a(K, P, xT, xB, modT, gain, C, W, pk):
    S = K.S
    constB = P["constB"]
    sh1 = modT[:, 0:16]
    sc1 = modT[:, 16:32]
    gm = K.sbuf("a_gm", [128, NFC], F32)
    gmB = Buf("gm")
    S.op("vector", lambda e: e.scalar_tensor_tensor(out=gm[:, :], in0=sc1, scalar=1.0, in1=gain[:, :], op0=ALU.add, op1=ALU.mult), reads=[constB], writes=[gmB])
    hT = K.sbuf("a_hT", [128, NFC, NSEG, SEG], BF16)
    hB = [[Buf("h%d_%d" % (fc, s)) for s in range(NSEG)] for fc in range(NFC)]
    rstd_t = K.sbuf("a_rstd", [128, NT], F32)
    rms_mod_to_bf16(K, xT, xB, gm, sh1, hT, hB, P["ones_bf"], P["ps_up"], P["acc"], P["sq"], rstd_t, extra=[gmB, constB])

    sv = K.sbuf("a_sv", [128, 16], F32)
    svB = Buf("sv")
    K.load(sv[:, :], W["svec"], [svB])
    stg = P["stg"]
    wfm = WStream(K, P["wup"], [W["w_fm"][i].rearrange("p (k j) -> p k j", j=128) for i in range(NFM)])
    out_bufs = []

    def fm_mm(ci, s, m):
        wt, wB = wfm.get(ci)
        ps, psB = P["ps_up"].next()
        for kc in range(NFC):
            S.op("tensor", lambda e, kc=kc: e.matmul(ps[0:m, :], wt[:, kc, 0:m], hT[:, kc, s, :], start=(kc == 0), stop=(kc == NFC - 1)),
                 reads=[wB, hB[kc][s]], writes=[psB])
        return ps, psB

    def store(dst_ap, src_ap, srcB):
        oB = Buf("o")
        out_bufs.append(oB)
        S.dma("sync", lambda e: e.dma_start(out=dst_ap, in_=src_ap), reads=[srcB], writes=[oB])

    def sl(s):
        return slice(s * SEG, (s + 1) * SEG)

    for ci in range(8):
        for s in range(NSEG):
            ps, psB = fm_mm(ci, s, 128)
            t, tB = stg.next()
            if ci < 4:
                S.op("scalar", lambda e, t=t, ps=ps: e.activation(out=t[:, :], in_=ps[:, :], func=AF.Copy, scale=0.125), reads=[psB], writes=[tB])
                store(pk["sb_q"][ci, :, sl(s)], t[:, :], tB)
            else:
                S.op("vector", lambda e, t=t, ps=ps: e.tensor_copy(out=t[:, :], in_=ps[:, :]), reads=[psB], writes=[tB])
                store(pk["sb_k"][ci - 4, :, sl(s)], t[:, :], tB)
    for i in range(4):
        pas = [fm_mm(8 + 2 * i, s, 128) for s in range(NSEG)]
        pgs = [fm_mm(9 + 2 * i, s, 128) for s in range(NSEG)]
        for s in range(NSEG):
            pa, paB = pas[s]
            pg, pgB = pgs[s]
            sg, sgB = P["acc"].next()
            S.op("scalar", lambda e, sg=sg, pg=pg: e.activation(out=sg[:, :], in_=pg[:, :], func=AF.Sigmoid), reads=[pgB], writes=[sgB])
            t, tB = stg.next()
            S.op("vector", lambda e, t=t, pa=pa, sg=sg: e.tensor_tensor(out=t[:, :], in0=pa[:, :], in1=sg[:, :], op=ALU.mult), reads=[paB, sgB], writes=[tB])
            store(pk["cv"][i, :, sl(s)], t[:, :], tB)
    for j in range(16):
        ci = 16 + j
        for s in range(NSEG):
            ps, psB = fm_mm(ci, s, 64)
            t, tB = stg.next()
            if j < 14:
                gcol = 0 if j < 8 else 1 + (j - 8) // 2
                head_norm_rope(K, P, ps[0:64, :], psB, 64, sv[0:64, gcol:gcol + 1], svB, 1.0 / 64, C["R64"], C["cosn"][0:64, sl(s)], C["sinn"][0:64, sl(s)],
                               t[0:64, :], tB, constB)
                if j < 8:
                    store(pk["nsa_q"][:, j, sl(s)], t[0:64, :], tB)
                else:
                    store(pk["nsa_k"][(j - 8) // 2, :, (j - 8) % 2, sl(s)], t[0:64, :], tB)
            else:
                S.op("vector", lambda e, t=t, ps=ps: e.tensor_copy(out=t[0:64, :], in_=ps[0:64, :]), reads=[psB], writes=[tB])
                store(pk["nsa_vc"][:, j - 14, sl(s)], t[0:64, :], tB)
    with K.scope():
        qln = K.sbuf("a_qln", [128, 3, NSEG, SEG], BF16)
        kvln = K.sbuf("a_kvln", [128, 2, NSEG, SEG], BF16)
        qlnB = [Buf("qln%d" % s) for s in range(NSEG)]
        kvlnB = [Buf("kvln%d" % s) for s in range(NSEG)]
        raw = K.sbuf("a_raw", [128, 3, NSEG, SEG], F32)
        rawB = Buf("raw")
        for (c0, nch, dst, dstB, gc0) in ((32, 3, qln, qlnB, 4), (35, 2, kvln, kvlnB, 7)):
            sss = [P["ps_dn"].next() for _ in range(NSEG)]
            for i in range(nch):
                for s in range(NSEG):
                    ss, ssB = sss[s]
                    ps, psB = fm_mm(c0 + i, s, 128)
                    S.op("scalar", lambda e, ps=ps, i=i, s=s: e.activation(out=raw[:, i, s, :], in_=ps[:, :], func=AF.Copy), reads=[psB], writes=[rawB])
                    sq, sqB = P["sq"].next()
                    S.op("scalar", lambda e, ps=ps, sq=sq: e.activation(out=sq[:, :], in_=ps[:, :], func=AF.Square), reads=[psB], writes=[sqB])
                    S.op("tensor", lambda e, sq=sq, ss=ss, i=i, nch=nch: e.matmul(ss[:, :], P["ones_bf"][:, :], sq[:, :], start=(i == 0), stop=(i == nch - 1)), reads=[sqB, constB], writes=[ssB])
            for s in range(NSEG):
                ss, ssB = sss[s]
                rs, rsB = P["acc"].next()
                S.op("scalar", lambda e, rs=rs, ss=ss, nch=nch: e.activation(out=rs[:, :], in_=ss[:, :], func=AF.Sqrt, scale=1.0 / (128 * nch), bias=K.eps_t[:, 0:1]), reads=[ssB, K.epsB], writes=[rsB])
                S.op("vector", lambda e, rs=rs: e.reciprocal(out=rs[:, :], in_=rs[:, :]), reads=[rsB], writes=[rsB])
                for i in range(nch):
                    S.op("vector", lambda e, i=i, rs=rs, dst=dst, s=s, gc0=gc0: e.scalar_tensor_tensor(out=dst[:, i, s, :], in0=raw[:, i, s, :], scalar=sv[:, gc0 + i:gc0 + i + 1], in1=rs[:, :], op0=ALU.mult, op1=ALU.mult),
                         reads=[rawB, rsB, svB], writes=[dstB[s]])
        kr = K.sbuf("a_kr", [32, NSEG, SEG], F32)
        krB = [Buf("kr%d" % s) for s in range(NSEG)]
        for s in range(NSEG):
            ps, psB = fm_mm(37, s, 32)
            S.op("scalar", lambda e, ps=ps, s=s: e.activation(out=kr[:, s, :], in_=ps[0:32, :], func=AF.Copy), reads=[psB], writes=[krB[s]])
        wq = K.sbuf("a_wq", [128, 3, 768], BF16)
        wk = K.sbuf("a_wk", [128, 2, 8, 96], BF16)
        wv = K.sbuf("a_wv", [128, 2, 512], BF16)
        wmB = Buf("wm")
        K.load(wq[:, :, :], W["w_uq"].rearrange("p (k j) -> p k j", j=768), [wmB], eng="gpsimd")
        K.load(wk[:, :, :, :], W["w_uk"].rearrange("p (k h j) -> p k h j", k=2, h=8), [wmB], eng="gpsimd")
        K.load(wv[:, :, :], W["w_uv"].rearrange("p (k j) -> p k j", j=512), [wmB], eng="gpsimd")
        for h in range(8):
            for s in range(NSEG):
                ps, psB = P["ps_up"].next()
                for kc in range(3):
                    S.op("tensor", lambda e, ps=ps, kc=kc, h=h, s=s: e.matmul(ps[0:96, :], wq[:, kc, h * 96:(h + 1) * 96], qln[:, kc, s, :], start=(kc == 0), stop=(kc == 2)),
                         reads=[wmB, qlnB[s]], writes=[psB])
                t, tB = stg.next()
                head_norm_rope(K, P, ps[0:96, :], psB, 96, sv[0:96, 9:10], svB, 1.0 / 96, C["R96"], C["cosm"][0:96, sl(s)], C["sinm"][0:96, sl(s)], t[0:96, :], tB, constB)
                store(pk["mla_q"][h, :, sl(s)], t[0:96, :], tB)
        for h in range(8):
            for s in range(NSEG):
                ps, psB = P["ps_up"].next()
                for kc in range(2):
                    S.op("tensor", lambda e, ps=ps, kc=kc, h=h, s=s: e.matmul(ps[0:96, :], wk[:, kc, h, :], kvln[:, kc, s, :], start=(kc == 0), stop=False),
                         reads=[wmB, kvlnB[s]], writes=[psB])
                S.op("tensor", lambda e, ps=ps, s=s: e.matmul(ps[0:96, :], C["sel"], kr[:, s, :], start=False, stop=True), reads=[krB[s], constB], writes=[psB])
                t, tB = stg.next()
                head_norm_rope(K, P, ps[0:96, :], psB, 96, sv[0:96, 10:11], svB, 1.0 / 96, C["R96"], C["cosm"][0:96, sl(s)], C["sinm"][0:96, sl(s)], t[0:96, :], tB, constB)
                store(pk["mla_k"][h, :, sl(s)], t[0:96, :], tB)
        for tt in range(NT // 128):
            s, o = divmod(tt * 128, SEG)
            ps, psB = P["ps_up"].next()
            for kc in range(2):
                S.op("tensor", lambda e, ps=ps, kc=kc, s=s, o=o: e.matmul(ps[:, :], kvln[:, kc, s, o:o + 128], wv[:, kc, :], start=(kc == 0), stop=(kc == 1)),
                     reads=[wmB, kvlnB[s]], writes=[psB])
            t, tB = stg.next()
            S.op("vector", lambda e, t=t, ps=ps: e.tensor_copy(out=t[:, :], in_=ps[:, :]), reads=[psB], writes=[tB])
            store(pk["mla_v"][tt * 128:(tt + 1) * 128, :], t[:, :], tB)
    with K.scope():
        wtm = K.sbuf("a_wtm", [128, NFC, TMW], BF16)
        wtmB = Buf("wtm")
        for kc in range(0, NFC, 4):
            K.load(wtm[:, kc:kc + 4, :], W["w_tm"][:, kc * TMW:(kc + 4) * TMW].rearrange("p (k j) -> p k j", j=TMW), [wtmB], eng="gpsimd")
        for tt in range(NT // 128):
            s, o = divmod(tt * 128, SEG)
            p1, p1B = P["ps_up"].next()
            p2, p2B = P["ps_up"].next()
            for kc in range(NFC):
                S.op("tensor", lambda e, p1=p1, kc=kc, s=s, o=o: e.matmul(p1[:, :], hT[:, kc, s, o:o + 128], wtm[:, kc, 0:512], start=(kc == 0), stop=(kc == NFC - 1)),
                     reads=[wtmB, hB[kc][s]], writes=[p1B])
                S.op("tensor", lambda e, p2=p2, kc=kc, s=s, o=o: e.matmul(p2[:, 0:280], hT[:, kc, s, o:o + 128], wtm[:, kc, 512:792], start=(kc == 0), stop=(kc == NFC - 1)),
                     reads=[wtmB, hB[kc][s]], writes=[p2B])
            t, tB = stg.next()
            S.op("vector", lambda e, t=t, p1=p1: e.tensor_copy(out=t[:, :], in_=p1[:, :]), reads=[p1B], writes=[tB])
            store(pk["sb_v"][tt * 128:(tt + 1) * 128, :], t[:, :], tB)
            t2, t2B = stg.next()
            S.op("scalar", lambda e, t2=t2, p2=p2: e.activation(out=t2[:, 0:256], in_=p2[:, 0:256], func=AF.Copy), reads=[p2B], writes=[t2B])
            store(pk["nsa_v"][tt * 128:(tt + 1) * 128, :], t2[:, 0:256], t2B)
            gt, gtB = P["acc"].next()
            S.op("scalar", lambda e, gt=gt, p2=p2: e.activation(out=gt[:, 0:24], in_=p2[:, 256:280], func=AF.Sigmoid), reads=[p2B], writes=[gtB])
            store(pk["gates"][tt * 128:(tt + 1) * 128, :], gt[:, 0:24], gtB)
    return out_bufs


NQT = NT // 128
NKT = 16


def group_norm_T(K, P, C, ybuf, yB, i, grp, gn, ynT, ynB):
    import os
    step = int(os.environ.get("GNSTEP", "9"))
    S = K.S
    junk, jB = P["acc"].next()
    ssq, sB = P["small"].next()
    S.op("scalar", lambda e: e.activation(out=junk[:, :], in_=ybuf[:, i, :], func=AF.Square, accum_out=ssq[:, 0:1]), reads=[yB[i]], writes=[jB, sB])
    if step < 2:
        return
    S.op("scalar", lambda e: e.activation(out=ssq[:, 1:2], in_=ssq[:, 0:1], func=AF.Sqrt, scale=1.0 / 512, bias=K.eps_t[:, 0:1]), reads=[sB, K.epsB], writes=[sB])
    S.op("vector", lambda e: e.reciprocal(out=ssq[:, 2:3], in_=ssq[:, 1:2]), reads=[sB], writes=[sB])
    if step < 3:
        return
    yb, ybB = P["acc"].next()
    S.op("vector", lambda e: e.tensor_scalar(out=yb[:, :], in0=ybuf[:, i, :], scalar1=ssq[:, 2:3], scalar2=None, op0=ALU.mult), reads=[yB[i], sB], writes=[ybB])
    if step < 4:
        return
    pt, ptB = P["ps_up"].next()
    for c in range(4):
        S.op("tensor", lambda e, c=c: e.transpose(pt[:, c * 128:(c + 1) * 128], yb[:, c * 128:(c + 1) * 128], C["ident32"]), reads=[ybB, P["constB"]], writes=[ptB])
    if step < 5:
        return
    for c in range(4):
        fc = 4 * grp + c
        if (c % 2 == 0 and os.environ.get('GNEV', 'both') == 'both') or os.environ.get('GNEV') == 'act':
            S.op("scalar", lambda e, c=c, fc=fc: e.activation(out=ynT[:, fc, i * 128:(i + 1) * 128], in_=pt[:, c * 128:(c + 1) * 128], func=AF.Identity, scale=gn[:, fc:fc + 1]),
                 reads=[ptB, P["constB"]], writes=[ynB[fc][i]])
        else:
            S.op("vector", lambda e, c=c, fc=fc: e.tensor_scalar(out=ynT[:, fc, i * 128:(i + 1) * 128], in0=pt[:, c * 128:(c + 1) * 128], scalar1=gn[:, fc:fc + 1], scalar2=None, op0=ALU.mult),
                 reads=[ptB, P["constB"]], writes=[ynB[fc][i]])


def bc4(ap, n):
    return ap.unsqueeze(1).broadcast_to([ap.shape[0], 4, n])


def mixer_mla(K, P, C, own, kv, ybuf, yB):
    S = K.S
    constB = P["constB"]
    scale = 96 ** -0.5
    kT = K.sbuf("m_kT", [96, 4, 2048], BF16)
    va = K.sbuf("m_va", [128, NKT, 4, 65], BF16)
    q = K.sbuf("m_q", [96, 4, NT], BF16)
    kTB, vaB, qB = Buf("kT"), Buf("va"), Buf("q")
    S.op("gpsimd", lambda e: e.memset(va[:, :, :, 64:65], 1.0), writes=[vaB])
    for hg in range(2):
        for h4 in range(4):
            K.load(kT[:, h4, :], kv["mla_k"][hg * 4 + h4], [kTB])
            K.load(q[:, h4, :], own["mla_q"][hg * 4 + h4], [qB])
        for h4 in range(4):
            K.load(va[:, :, h4, 0:64], kv["mla_v"][:, hg * 256 + h4 * 64:hg * 256 + (h4 + 1) * 64].rearrange("(t p) d -> p t d", p=128), [vaB])
        for i in range(NQT):
            O, OB = P["ps_dn"].next()
            nk = 2 * i + 2
            for kt in range(nk):
                Sb, SB_ = P["ps_up"].next()
                masked = kt >= 2 * i
                for h4 in range(4):
                    S.op("tensor", lambda e, h4=h4, kt=kt, Sb=Sb, i=i: e.matmul(Sb[:, h4 * 128:(h4 + 1) * 128], kT[:, h4, kt * 128:(kt + 1) * 128], q[:, h4, i * 128:(i + 1) * 128],
                                                                                 start=(h4 == 0), stop=(h4 == 3 and not masked)), reads=[kTB, qB], writes=[SB_])
                if masked:
                    mk = C["mA"] if kt == 2 * i else C["mB"]
                    S.op("tensor", lambda e, Sb=Sb, mk=mk: e.matmul(Sb[:, :].rearrange("p (a b) -> p a b", a=4), C["ident"], bc4(mk, 128), start=False, stop=True), reads=[constB], writes=[SB_])
                Pt, PtB = P["pt"].next()
                S.op("scalar", lambda e, Pt=Pt, Sb=Sb: e.activation(out=Pt[:, :], in_=Sb[:, :], func=AF.Exp, scale=scale), reads=[SB_], writes=[PtB])
                for h4 in range(4):
                    S.op("tensor", lambda e, h4=h4, kt=kt, Pt=Pt, O=O, nk=nk: e.matmul(O[:, h4 * 65:(h4 + 1) * 65], Pt[:, h4 * 128:(h4 + 1) * 128], va[:, kt, h4, :], start=(kt == 0 and h4 == 0), stop=(kt == nk - 1 and h4 == 3)),
                         reads=[PtB, vaB], writes=[OB])
            rz, rzB = P["small"].next()
            Ov = O[:, 0:260].rearrange("p (h d) -> p h d", d=65)
            S.op("vector", lambda e, rz=rz, Ov=Ov: e.reciprocal(out=rz[:, 0:4], in_=Ov[:, :, 64]), reads=[OB], writes=[rzB])
            S.op("vector", lambda e, rz=rz, Ov=Ov, i=i, hg=hg: e.tensor_tensor(out=ybuf[:, i, hg * 256:(hg + 1) * 256].rearrange("p (h d) -> p h d", d=64), in0=Ov[:, :, 0:64],
                                                                                in1=rz[:, 0:4].unsqueeze(2).broadcast_to([128, 4, 64]), op=ALU.mult), reads=[OB, rzB], writes=[yB[i]])


def mixer_sb(K, P, C, own, kv, ybuf, yB):
    S = K.S
    import os
    step = int(os.environ.get("SBSTEP", "9"))
    constB = P["constB"]
    kT = K.sbuf("s_kT", [64, 4, 2048], BF16)
    v = K.sbuf("s_v", [128, NKT, 256], BF16)
    q = K.sbuf("s_q", [64, 4, NT], BF16)
    kTB, vB, qB = Buf("kT"), Buf("v"), Buf("q")
    R = K.sbuf("s_R", [128, 512], F32)
    RB = Buf("R")
    for hg in range(2):
        for h4 in range(4):
            K.load(kT[:, h4, :], kv["sb_k"][hg * 2 + h4 // 2, 64 * (h4 % 2):64 * (h4 % 2) + 64, :], [kTB])
            K.load(q[:, h4, :], own["sb_q"][hg * 2 + h4 // 2, 64 * (h4 % 2):64 * (h4 % 2) + 64, :], [qB])
        K.load(v[:, :, :], kv["sb_v"][:, hg * 256:(hg + 1) * 256].rearrange("(t p) c -> p t c", p=128), [vB])
        for i in range(NQT):
            O, OB = P["ps_dn"].next()
            nk = 2 * i + 2
            S.op("vector", lambda e: e.memset(R[:, :], 0.0), writes=[RB])
            for kt in range(nk - 1, -1, -1):
                if step < 1:
                    continue
                masked = kt >= 2 * i
                zb, zbB = P["ps_up"].next()
                Z, ZB = P["ps_up"].next()
                for h4 in range(4):
                    c, pb = h4 // 2, 64 * (h4 % 2)
                    S.op("tensor", lambda e, h4=h4, c=c, pb=pb, kt=kt, zb=zb, i=i: e.matmul(zb[:, h4 * 128:(h4 + 1) * 128], kT[:, h4, kt * 128:(kt + 1) * 128], q[:, h4, i * 128:(i + 1) * 128],
                                                                                          start=(h4 == 0), stop=(h4 == 3)), reads=[kTB, qB], writes=[zbB])
                for h4 in range(4):
                    c, pb = h4 // 2, 64 * (h4 % 2)
                    S.op("tensor", lambda e, h4=h4, c=c, pb=pb, kt=kt, Z=Z, i=i: e.matmul(Z[:, h4 * 128:(h4 + 1) * 128], kT[:, h4, kt * 128:(kt + 1) * 128], q[:, h4, i * 128:(i + 1) * 128],
                                                                                         start=(h4 == 0), stop=False), reads=[kTB, qB], writes=[ZB])
                E, EB = P["acc"].next()
                S.op("scalar", lambda e, E=E, zb=zb: e.activation(out=E[:, :], in_=zb[:, :], func=AF.Exp), reads=[zbB], writes=[EB])
                if step < 2:
                    continue
                sp, spB = P["pt"].next()
                S.op("scalar", lambda e, E=E, sp=sp: e.activation(out=sp[:, :], in_=E[:, :], func=AF.Ln, bias=K.eps_t[:, 2:3], scale=1.0), reads=[EB, K.epsB], writes=[spB])
                if masked:
                    mk = C["sbA"] if kt == 2 * i else C["sbB"]
                    S.op("vector", lambda e, sp=sp, mk=mk: e.tensor_tensor(out=sp[:, :].rearrange("p (a b) -> p a b", a=4), in0=sp[:, :].rearrange("p (a b) -> p a b", a=4), in1=bc4(mk, 128), op=ALU.mult),
                         reads=[spB, constB], writes=[spB])
                if step < 3:
                    continue
                S.op("tensor", lambda e, Z=Z, sp=sp: e.matmul(Z[:, :], C["negtri"], sp[:, :], start=False, stop=True), reads=[spB, constB], writes=[ZB])
                L, LB = P["acc"].next()
                S.op("vector", lambda e, L=L, Z=Z: e.tensor_tensor(out=L[:, :], in0=Z[:, :], in1=R[:, :], op=ALU.subtract), reads=[ZB, RB], writes=[LB])
                if step < 4:
                    continue
                A, AB = P["pt"].next()
                S.op("scalar", lambda e, A=A, L=L: e.activation(out=A[:, :], in_=L[:, :], func=AF.Exp), reads=[LB], writes=[AB])
                if masked:
                    S.op("vector", lambda e, A=A, mk=mk: e.tensor_tensor(out=A[:, :].rearrange("p (a b) -> p a b", a=4), in0=A[:, :].rearrange("p (a b) -> p a b", a=4), in1=bc4(mk, 128), op=ALU.mult),
                         reads=[AB, constB], writes=[AB])
                if step < 5:
                    continue
                if kt > 0:
                    Y, YB = P["ps_up"].next()
                    S.op("tensor", lambda e, Y=Y, sp=sp: e.matmul(Y[:, :], P["ones_bf"][:, :], sp[:, :], start=True, stop=True), reads=[spB, constB], writes=[YB])
                    S.op("vector", lambda e, Y=Y: e.tensor_tensor(out=R[:, :], in0=R[:, :], in1=Y[:, :], op=ALU.add), reads=[YB, RB], writes=[RB])
                if step < 6:
                    continue
                for h4 in range(4):
                    S.op("tensor", lambda e, h4=h4, kt=kt, A=A, O=O, nk=nk: e.matmul(O[:, h4 * 64:(h4 + 1) * 64], A[:, h4 * 128:(h4 + 1) * 128], v[:, kt, h4 * 64:(h4 + 1) * 64], start=(kt == nk - 1 and h4 == 0), stop=(kt == 0 and h4 == 3)),
                         reads=[AB, vB], writes=[OB])
            if step < 6:
                S.op("vector", lambda e, i=i, hg=hg: e.memset(ybuf[:, i, hg * 256:(hg + 1) * 256], 1.0), writes=[yB[i]])
                continue
            S.op("scalar", lambda e, O=O, i=i, hg=hg: e.activation(out=ybuf[:, i, hg * 256:(hg + 1) * 256], in_=O[:, 0:256], func=AF.Copy), reads=[OB], writes=[yB[i]])


def mixer_conv(K, P, C, W, cvh, kv_own_cv, ybuf, yB):
    S = K.S
    constB = P["constB"]
    TW = 158
    gb = K.sbuf("c_gb", [128, 4, NQT, TW], BF16)
    gbB = Buf("gb")
    for c in range(4):
        K.load(gb[:, c, :, 0:30], cvh[c], [gbB])
        K.load(gb[:, c, :, 30:TW], kv_own_cv[c].rearrange("p (t n) -> p t n", n=128), [gbB])
    cp = K.sbuf("c_par", [128, 4, 36], F32)
    cpB = Buf("cp")
    K.load(cp[:, :, :], W["conv_par"], [cpB])
    pw = K.sbuf("c_pw", [128, 4, 512], BF16)
    pwB = Buf("pw")
    K.load(pw[:, :, :], W["conv_pw"].rearrange("p (k j) -> p k j", j=512), [pwB], eng="gpsimd")
    pwb = K.sbuf("c_pwb", [128, 512], F32)
    K.load(pwb[:, :], W["conv_pwb"], [pwB])
    hc = K.sbuf("c_hc", [128, 4, 512], F32)
    st = K.sbuf("c_st", [128, 4, 512], BF16)
    hcB, stB = Buf("hc"), Buf("st")
    dg = K.sbuf("c_dg", [128, 31, 128], BF16)
    dgB = Buf("dg")
    for half in range(2):
        s1, s1B = P["ps_dn"].next()
        s2, s2B = P["ps_dn"].next()
        for c in range(4):
            S.op("vector", lambda e, c=c: e.tensor_tensor(out=dg[:, :, :], in0=C["ident"].unsqueeze(1).broadcast_to([128, 31, 128]),
                                                           in1=cp[:, c, 0:31].unsqueeze(2).broadcast_to([128, 31, 128]), op=ALU.mult), reads=[constB, cpB], writes=[dgB])
            ps, psB = P["ps_up"].next()
            for k in range(31):
                S.op("tensor", lambda e, c=c, k=k, ps=ps, half=half: e.matmul(ps[:, :].rearrange("p (a b) -> p a b", a=4), dg[:, k, :], gb[:, c, half * 4:(half + 1) * 4, k:k + 128], start=(k == 0), stop=(k == 30)),
                     reads=[dgB, gbB], writes=[psB])
            S.op("scalar", lambda e, c=c, ps=ps: e.activation(out=hc[:, c, :], in_=ps[:, :], func=AF.Identity, bias=cp[:, c, 31:32], scale=1.0), reads=[psB, cpB], writes=[hcB])
            hb, hbB = P["pt"].next()
            S.op("vector", lambda e, c=c, hb=hb: e.tensor_copy(out=hb[:, :], in_=hc[:, c, :]), reads=[hcB], writes=[hbB])
            hq, hqB = P["pt"].next()
            S.op("scalar", lambda e, c=c, ps=ps, hq=hq: e.activation(out=hq[:, :], in_=ps[:, :], func=AF.Square, bias=cp[:, c, 31:32], scale=1.0), reads=[psB, cpB], writes=[hqB])
            S.op("tensor", lambda e, c=c, s1=s1, hb=hb: e.matmul(s1[:, :], P["ones_bf"][:, :], hb[:, :], start=(c == 0), stop=(c == 3)), reads=[hbB, constB], writes=[s1B])
            S.op("tensor", lambda e, c=c, s2=s2, hq=hq: e.matmul(s2[:, :], P["ones_bf"][:, :], hq[:, :], start=(c == 0), stop=(c == 3)), reads=[hqB, constB], writes=[s2B])
        mu, muB = P["acc"].next()
        S.op("scalar", lambda e, mu=mu, s1=s1: e.activation(out=mu[:, :], in_=s1[:, :], func=AF.Copy, scale=1.0 / 512), reads=[s1B], writes=[muB])
        msq, msqB = P["acc"].next()
        S.op("vector", lambda e, mu=mu, msq=msq: e.tensor_tensor(out=msq[:, :], in0=mu[:, :], in1=mu[:, :], op=ALU.mult), reads=[muB], writes=[msqB])
        S.op("vector", lambda e, msq=msq, s2=s2: e.scalar_tensor_tensor(out=msq[:, :], in0=s2[:, :], scalar=1.0 / 512, in1=msq[:, :], op0=ALU.mult, op1=ALU.subtract), reads=[s2B, msqB], writes=[msqB])
        S.op("scalar", lambda e, msq=msq: e.activation(out=msq[:, :], in_=msq[:, :], func=AF.Sqrt, bias=K.eps_t[:, 1:2], scale=1.0), reads=[msqB, K.epsB], writes=[msqB])
        S.op("vector", lambda e, msq=msq: e.reciprocal(out=msq[:, :], in_=msq[:, :]), reads=[msqB], writes=[msqB])
        for c in range(4):
            d, dB = P["acc"].next()
            S.op("vector", lambda e, c=c, d=d, mu=mu: e.tensor_tensor(out=d[:, :], in0=hc[:, c, :], in1=mu[:, :], op=ALU.subtract), reads=[hcB, muB], writes=[dB])
            S.op("vector", lambda e, d=d, msq=msq: e.tensor_tensor(out=d[:, :], in0=d[:, :], in1=msq[:, :], op=ALU.mult), reads=[dB, msqB], writes=[dB])
            S.op("scalar", lambda e, c=c, d=d: e.activation(out=st[:, c, :], in_=d[:, :], func=AF.Silu, scale=cp[:, c, 32:33], bias=cp[:, c, 33:34]), reads=[dB, cpB], writes=[stB])
        for j in range(4):
            i = half * 4 + j
            po, poB = P["ps_up"].next()
            for c in range(4):
                S.op("tensor", lambda e, c=c, j=j, po=po: e.matmul(po[:, :], st[:, c, j * 128:(j + 1) * 128], pw[:, c, :], start=(c == 0), stop=(c == 3)), reads=[stB, pwB], writes=[poB])
            S.op("vector", lambda e, po=po, i=i: e.tensor_tensor(out=ybuf[:, i, :], in0=po[:, :], in1=pwb[:, :], op=ALU.add), reads=[poB, pwB], writes=[yB[i]])


def mixer_nsa(K, P, C, W, own, kv, ybuf, yB):
    S = K.S
    constB = P["constB"]
    kc = K.sbuf("n_kc", [128, 2048], BF16)
    ks = K.sbuf("n_ks", [128, 2048], BF16)
    kw = K.sbuf("n_kw", [128, 2048], BF16)
    vcT = K.sbuf("n_vc", [128, 2048], BF16)
    vsa = K.sbuf("n_vsa", [128, NKT, 2, 65], BF16)
    vwa = K.sbuf("n_vwa", [128, NKT, 2, 65], BF16)
    qn = K.sbuf("n_q", [128, 4, NT], BF16)
    gt = K.sbuf("n_gt", [128, NQT, 24], F32)
    ldB = Buf("nsa_ld")
    S.op("gpsimd", lambda e: e.memset(vsa[:, :, :, 64:65], 1.0), writes=[ldB])
    S.op("gpsimd", lambda e: e.memset(vwa[:, :, :, 64:65], 1.0), writes=[ldB])
    for g in range(2):
        K.load(kc[64 * g:64 * g + 64, :], kv["nsa_k"][0, :, g, :], [ldB])
        K.load(ks[64 * g:64 * g + 64, :], kv["nsa_k"][1, :, g, :], [ldB])
        K.load(kw[64 * g:64 * g + 64, :], kv["nsa_k"][2, :, g, :], [ldB])
        K.load(vcT[64 * g:64 * g + 64, :], kv["nsa_vc"][:, g, :], [ldB])
        K.load(qn[64 * g:64 * g + 64, :, :], own["nsa_q"][:, 4 * g:4 * g + 4, :], [ldB])
    for g in range(2):
        K.load(vsa[:, :, g, 0:64], kv["nsa_v"][:, 64 * g:64 * g + 64].rearrange("(t p) d -> p t d", p=128), [ldB])
        K.load(vwa[:, :, g, 0:64], kv["nsa_v"][:, 128 + 64 * g:128 + 64 * g + 64].rearrange("(t p) d -> p t d", p=128), [ldB])
    K.load(gt[:, :, :], own["gates"].rearrange("(t p) c -> p t c", p=128), [ldB])
    cwk = K.sbuf("n_cwk", [128, 32, 128], BF16)
    cwv = K.sbuf("n_cwv", [128, 32, 64], BF16)
    peT = K.sbuf("n_pe", [128, 2, 32], BF16)
    K.load(cwk[:, :, :], W["cmp_wk"].rearrange("p (l d) -> p l d", l=32), [ldB], eng="gpsimd")
    K.load(cwv[:, :, :], W["cmp_wv"].rearrange("p (l d) -> p l d", l=32), [ldB], eng="gpsimd")
    K.load(peT[:, :, :], W["cmp_peT"].rearrange("p (a l) -> p a l", a=2), [ldB], eng="gpsimd")
    kcm = K.sbuf("n_kcm", [128, 128], BF16)
    crhs = K.sbuf("n_crhs", [128, 2, 97], BF16)
    cmB = Buf("cm")
    S.op("gpsimd", lambda e: e.memset(crhs[:, :, 64:65], 1.0), writes=[cmB])
    sm = K.sbuf("n_sm", [128, 4], F32)
    smb = K.sbuf("n_smb", [128, 64], BF16)
    smB = Buf("sm")
    ps, psB = P["ps_up"].next()
    for l in range(32):
        S.op("tensor", lambda e, l=l, ps=ps: e.matmul(ps[:, 0:127], cwk[:, l, :], kc[:, l:l + 2017:16], start=(l == 0), stop=(l == 31)), reads=[ldB], writes=[psB])
    pb, pbB = P["ps_dn"].next()
    for l in range(32):
        S.op("tensor", lambda e, l=l, pb=pb: e.matmul(pb[:, 0:1], cwk[:, l, :], peT[:, 0, l:l + 1], start=(l == 0), stop=(l == 31)), reads=[ldB], writes=[pbB])
    S.op("vector", lambda e, pb=pb: e.tensor_copy(out=sm[:, 0:1], in_=pb[:, 0:1]), reads=[pbB], writes=[smB])
    S.op("scalar", lambda e, ps=ps: e.activation(out=kcm[:, 0:127], in_=ps[:, 0:127], func=AF.Identity, bias=sm[:, 0:1], scale=1.0), reads=[psB, smB], writes=[cmB])
    pr, prB = P["ps_dn"].next()
    for l in range(32):
        S.op("tensor", lambda e, l=l, pr=pr: e.matmul(pr[0:1, 0:64], peT[0:64, 1, l:l + 1], cwv[0:64, l, :], start=(l == 0), stop=(l == 31)), reads=[ldB], writes=[prB])
    S.op("vector", lambda e, pr=pr: e.tensor_copy(out=smb[0:1, 0:64], in_=pr[0:1, 0:64]), reads=[prB], writes=[smB])
    for g in range(2):
        pv, pvB = P["ps_up"].next()
        for l in range(32):
            S.op("tensor", lambda e, l=l, g=g, pv=pv: e.matmul(pv[0:127, 0:64], vcT[64 * g:64 * g + 64, l:l + 2017:16], cwv[64 * g:64 * g + 64, l, :], start=(l == 0), stop=False), reads=[ldB], writes=[pvB])
        S.op("tensor", lambda e, pv=pv: e.matmul(pv[0:127, 0:64], P["ones_bf"][0:1, 0:127], smb[0:1, 0:64], start=False, stop=True), reads=[smB, constB], writes=[pvB])
        S.op("vector", lambda e, pv=pv, g=g: e.tensor_copy(out=crhs[0:127, g, 0:64], in_=pv[0:127, 0:64]), reads=[pvB], writes=[cmB])
        S.op("vector", lambda e, g=g: e.tensor_copy(out=crhs[0:127, g, 65:97], in_=C["ov"][0:127, :]), reads=[constB], writes=[cmB])
    selT = K.sbuf("n_selT", [32, 128], BF16)
    selB = Buf("selT")
    for i in range(NQT):
        qs = slice(i * 128, (i + 1) * 128)
        for g in range(2):
            qv = qn[64 * g:64 * g + 64, :, qs]
            Sb, SbB = P["ps_up"].next()
            S.op("tensor", lambda e, Sb=Sb, g=g, qv=qv: e.matmul(Sb[0:127, :].rearrange("p (a b) -> p a b", a=4), kcm[64 * g:64 * g + 64, 0:127], qv, start=True, stop=False), reads=[cmB, ldB], writes=[SbB])
            S.op("tensor", lambda e, Sb=Sb, qs=qs: e.matmul(Sb[0:127, :].rearrange("p (a b) -> p a b", a=4), C["ident"][0:127, 0:127], bc4(C["cmpmask"][0:127, qs], 128), start=False, stop=True),
                 reads=[constB], writes=[SbB])
            Pc, PcB = P["pt"].next()
            S.op("scalar", lambda e, Pc=Pc, Sb=Sb: e.activation(out=Pc[0:127, :], in_=Sb[0:127, :], func=AF.Exp, scale=0.125), reads=[SbB], writes=[PcB])
            CO, COB = P["ps_dn"].next()
            for r in range(4):
                S.op("tensor", lambda e, r=r, Pc=Pc, CO=CO, g=g: e.matmul(CO[:, r * 97:(r + 1) * 97], Pc[0:127, r * 128:(r + 1) * 128], crhs[0:127, g, :], start=(r == 0), stop=(r == 3)), reads=[PcB, cmB], writes=[COB])
            COv = CO[:, 0:388].rearrange("p (h d) -> p h d", d=97)
            rz, rzB = P["small"].next()
            S.op("vector", lambda e, rz=rz, COv=COv: e.tensor_scalar(out=rz[:, 0:4], in0=COv[:, :, 64], scalar1=1e-30, scalar2=None, op0=ALU.max), reads=[COB], writes=[rzB])
            S.op("vector", lambda e, rz=rz: e.reciprocal(out=rz[:, 0:4], in_=rz[:, 0:4]), reads=[rzB], writes=[rzB])
            imp, impB = P["acc"].next()
            S.op("vector", lambda e, imp=imp, COv=COv, rz=rz: e.tensor_scalar(out=imp[:, 0:32], in0=COv[:, 0, 65:97], scalar1=rz[:, 0:1], scalar2=None, op0=ALU.mult), reads=[COB, rzB], writes=[impB])
            for r in range(1, 4):
                S.op("vector", lambda e, r=r, imp=imp, COv=COv, rz=rz: e.scalar_tensor_tensor(out=imp[:, 0:32], in0=COv[:, r, 65:97], scalar=rz[:, r:r + 1], in1=imp[:, 0:32], op0=ALU.mult, op1=ALU.add),
                     reads=[COB, rzB, impB], writes=[impB])
            S.op("vector", lambda e, imp=imp, i=i: e.tensor_tensor(out=imp[:, 0:32], in0=imp[:, 0:32], in1=C["tkbias"][:, i, :], op=ALU.add), reads=[impB, constB], writes=[impB])
            S.op("vector", lambda e, imp=imp: e.max(out=imp[:, 64:72], in_=imp[:, 0:32]), reads=[impB], writes=[impB])
            S.op("vector", lambda e, imp=imp: e.match_replace(out=imp[:, 32:64], in_to_replace=imp[:, 64:72], in_values=imp[:, 0:32], imm_value=-1e38), reads=[impB], writes=[impB])
            S.op("vector", lambda e, imp=imp: e.max(out=imp[:, 72:80], in_=imp[:, 32:64]), reads=[impB], writes=[impB])
            selm, selmB = P["acc"].next()
            S.op("vector", lambda e, imp=imp, selm=selm: e.tensor_scalar(out=selm[:, 0:32], in0=imp[:, 0:32], scalar1=imp[:, 79:80], scalar2=1.0, op0=ALU.is_ge, op1=ALU.subtract), reads=[impB], writes=[selmB])
            ptr, ptrB = P["ps_up"].next()
            S.op("tensor", lambda e, ptr=ptr, selm=selm: e.transpose(ptr[0:32, 0:128], selm[:, 0:32], C["ident32"]), reads=[selmB, constB], writes=[ptrB])
            S.op("vector", lambda e, ptr=ptr: e.tensor_copy(out=selT[:, :], in_=ptr[0:32, 0:128]), reads=[ptrB], writes=[selB])
            cf, cfB = P["small"].next()
            gsl = gt[:, i, 12 * g:12 * g + 12].rearrange("p (h c) -> p h c", c=3)
            S.op("vector", lambda e, cf=cf, gsl=gsl, rz=rz: e.tensor_tensor(out=cf[:, 0:4], in0=gsl[:, :, 0], in1=rz[:, 0:4], op=ALU.mult), reads=[ldB, rzB], writes=[cfB])
            ob, obB = P["acc"].next()
            obv = ob[:, 0:256].rearrange("p (h d) -> p h d", d=64)
            S.op("vector", lambda e, obv=obv, COv=COv, cf=cf: e.tensor_tensor(out=obv, in0=COv[:, :, 0:64], in1=cf[:, 0:4].unsqueeze(2).broadcast_to([128, 4, 64]), op=ALU.mult), reads=[COB, cfB], writes=[obB])
            for br in (1, 2):
                if br == 1:
                    kts = list(range(0, 2 * i + 2))
                    kk, va = ks, vsa
                    Ob, ObB = P["ps_dn"].next()
                else:
                    kts = [kt for kt in range(2 * i - 4, 2 * i + 2) if kt >= 0]
                    kk, va = kw, vwa
                    Ob, ObB = P["ps_o"].next()
                for n_, kt in enumerate(kts):
                    Sb, SbB = P["ps_up"].next()
                    Sv = Sb[:, :].rearrange("p (a b) -> p a b", a=4)
                    S.op("tensor", lambda e, Sv=Sv, kk=kk, g=g, kt=kt, qv=qv: e.matmul(Sv, kk[64 * g:64 * g + 64, kt * 128:(kt + 1) * 128], qv, start=True, stop=False), reads=[ldB], writes=[SbB])
                    if br == 1:
                        last = kt < 2 * i
                        S.op("tensor", lambda e, Sv=Sv, kt=kt, last=last: e.matmul(Sv, C["ebig"][0:32, kt * 128:(kt + 1) * 128], bc4(selT[:, :], 128), start=False, stop=last), reads=[selB, constB], writes=[SbB])
                        if not last:
                            mk = C["mA"] if kt == 2 * i else C["mB"]
                            S.op("tensor", lambda e, Sv=Sv, mk=mk: e.matmul(Sv, C["ident"], bc4(mk, 128), start=False, stop=True), reads=[constB], writes=[SbB])
                    else:
                        mk = C["mW"][:, (kt - (2 * i - 4)) * 128:(kt - (2 * i - 4) + 1) * 128]
                        S.op("tensor", lambda e, Sv=Sv, mk=mk: e.matmul(Sv, C["ident"], bc4(mk, 128), start=False, stop=True), reads=[constB], writes=[SbB])
                    Pt, PtB = P["pt"].next()
                    S.op("scalar", lambda e, Pt=Pt, Sb=Sb: e.activation(out=Pt[:, :], in_=Sb[:, :], func=AF.Exp, scale=0.125), reads=[SbB], writes=[PtB])
                    for r in range(4):
                        S.op("tensor", lambda e, r=r, Pt=Pt, Ob=Ob, va=va, kt=kt, g=g, n_=n_, nn=len(kts): e.matmul(Ob[:, r * 65:(r + 1) * 65], Pt[:, r * 128:(r + 1) * 128], va[:, kt, g, :], start=(n_ == 0 and r == 0), stop=(n_ == nn - 1 and r == 3)),
                             reads=[PtB, ldB], writes=[ObB])
                Ov = Ob[:, 0:260].rearrange("p (h d) -> p h d", d=65)
                rz2, rz2B = P["small"].next()
                S.op("vector", lambda e, rz2=rz2, Ov=Ov: e.reciprocal(out=rz2[:, 0:4], in_=Ov[:, :, 64]), reads=[ObB], writes=[rz2B])
                S.op("vector", lambda e, rz2=rz2, gsl=gsl, br=br: e.tensor_tensor(out=rz2[:, 4:8], in0=gsl[:, :, br], in1=rz2[:, 0:4], op=ALU.mult), reads=[ldB, rz2B], writes=[rz2B])
                tmp, tmpB = P["acc"].next()
                tv = tmp[:, 0:256].rearrange("p (h d) -> p h d", d=64)
                S.op("vector", lambda e, tv=tv, Ov=Ov, rz2=rz2: e.tensor_tensor(out=tv, in0=Ov[:, :, 0:64], in1=rz2[:, 4:8].unsqueeze(2).broadcast_to([128, 4, 64]), op=ALU.mult), reads=[ObB, rz2B], writes=[tmpB])
                if br == 1:
                    S.op("vector", lambda e, ob=ob, tmp=tmp: e.tensor_tensor(out=ob[:, 0:256], in0=ob[:, 0:256], in1=tmp[:, 0:256], op=ALU.add), reads=[obB, tmpB], writes=[obB])
                else:
                    S.op("vector", lambda e, ob=ob, tmp=tmp, i=i, g=g: e.tensor_tensor(out=ybuf[:, i, g * 256:(g + 1) * 256], in0=ob[:, 0:256], in1=tmp[:, 0:256], op=ALU.add), reads=[obB, tmpB], writes=[yB[i]])


def out_proj(K, P, W, ynT, ynB, xT, xB, g1):
    S = K.S
    ws = WStream(K, P["wup"], [W["w_o"][fc].rearrange("p (k j) -> p k j", j=128) for fc in range(NFC)])
    for fc in range(NFC):
        wt, wB = ws.get(fc)
        for s in range(NSEG):
            ps, psB = P["ps_dn"].next()
            for kc in range(NFC):
                S.op("tensor", lambda e, wt=wt, kc=kc, s=s, ps=ps: e.matmul(ps[:, :], wt[:, kc, :], ynT[:, kc, s * SEG:(s + 1) * SEG], start=(kc == 0), stop=(kc == NFC - 1)),
                     reads=[wB] + [ynB[kc][4 * s + j] for j in range(4)], writes=[psB])
            S.op("vector", lambda e, ps=ps, fc=fc, s=s: e.scalar_tensor_tensor(out=xT[:, fc, s * SEG:(s + 1) * SEG], in0=ps[:, :], scalar=g1[:, fc:fc + 1], in1=xT[:, fc, s * SEG:(s + 1) * SEG], op0=ALU.mult, op1=ALU.add),
                 reads=[psB, P["constB"]], writes=[xB[fc][s]])


def phase_b_mixers(K, P, C, W, own, kv, cvh, gn, ynT, ynB):
    ybuf = K.sbuf("b_ybuf", [128, NQT, 512], F32)
    yB = [Buf("y%d" % i) for i in range(NQT)]
    for grp, fn in enumerate((lambda: mixer_sb(K, P, C, own, kv, ybuf, yB),
                              lambda: mixer_conv(K, P, C, W, cvh, own["cv"], ybuf, yB),
                              lambda: mixer_nsa(K, P, C, W, own, kv, ybuf, yB),
                              lambda: mixer_mla(K, P, C, own, kv, ybuf, yB))):
        import os
        if os.environ.get("MIXERS") and str(grp) not in os.environ["MIXERS"]:
            continue
        with K.scope():
            fn()
            if os.environ.get("NOGN"):
                continue
            for i in range(NQT):
                group_norm_T(K, P, C, ybuf, yB, i, grp, gn, ynT, ynB)


def own_tokens(j):
    return np.concatenate([np.arange((2 * i + j) * 128, (2 * i + j + 1) * 128) for i in range(NQT)])


def feat_major(a):
    T, F = a.shape
    return np.ascontiguousarray(a.T.reshape(F // 128, 128, T).transpose(1, 0, 2))


def vec_fm(v):
    return np.ascontiguousarray(v.reshape(-1, 128).T)


def bf16_round(x):
    import ml_dtypes
    return np.asarray(x, np.float32).astype(ml_dtypes.bfloat16).astype(np.float32)


def make_consts(j):
    f = np.float32
    kp = np.arange(128)[:, None]
    qq = np.arange(128)[None, :]
    cbf = np.zeros((128, NBF), f)
    cbf[:, 0:128] = np.eye(128)
    cbf[:, 128:256] = np.where(kp >= qq, -1.0, 0.0)
    caus = np.where(kp > qq, NEG, 0.0)
    strict = np.where(kp < qq, 1.0, 0.0)
    edge = np.where(kp > qq, 0.0, NEG)
    full = np.full((128, 128), NEG)
    zero = np.zeros((128, 128))
    one = np.ones((128, 128))
    if j == 0:
        cbf[:, 256:384], cbf[:, 384:512] = caus, full
        cbf[:, 512:640], cbf[:, 640:768] = strict, zero
        mw = [edge, zero, zero, zero, caus, full]
    else:
        cbf[:, 256:384], cbf[:, 384:512] = zero, caus
        cbf[:, 512:640], cbf[:, 640:768] = one, strict
        mw = [full, edge, zero, zero, zero, caus]
    for p, m in enumerate(mw):
        cbf[:, 768 + p * 128:768 + (p + 1) * 128] = m
    tok = own_tokens(j)
    n = np.arange(128)[:, None]
    cbf[:, 1536:2560] = np.where((16 * n + 31 <= tok[None, :]) & (n < 127), 0.0, NEG)
    starts = np.arange(127) * 16
    ss = np.arange(32) * 64
    ov = np.clip(np.minimum(starts[:, None] + 32, ss[None, :] + 64) - np.maximum(starts[:, None], ss[None, :]), 0, None) / 32.0
    cbf[0:127, 2560:2592] = ov
    cebig = np.zeros((32, 2048), f)
    for b in range(32):
        cebig[b, b * 64:(b + 1) * 64] = -NEG
    cf = np.zeros((128, NF32), f)
    for m in range(64):
        cf[(m + 32) % 64, m] = 1.0
    for m in range(64, 80):
        cf[m + 16, 64 + m] = 1.0
    for m in range(80, 96):
        cf[m - 16, 64 + m] = 1.0
    for k in range(32):
        cf[k, 160 + 64 + k] = 1.0
    tk = np.zeros((128, NQT, 32), f)
    for i in range(NQT):
        t = tok[i * 128:(i + 1) * 128]
        cur = (t // 64)[:, None]
        blk = np.arange(32)[None, :]
        forced = (blk == 0) | (blk == cur) | (blk == cur - 1)
        elig = blk <= cur
        tk[:, i, :] = np.where(elig, np.where(forced, 1000.0, 0.0), -1e30)
    cf[:, 256:512] = tk.reshape(128, 256)
    rope = np.zeros((128, 4, NT), f)
    pos = tok.astype(f)
    inv = (10000.0 ** (-np.arange(0, 64, 2, dtype=f) / 64)).astype(f)
    ang = pos[:, None] * inv[None, :]
    c_, s_ = np.cos(ang).astype(f).T, np.sin(ang).astype(f).T
    rope[0:64, 0] = np.concatenate([c_, c_], 0)
    rope[0:64, 1] = np.concatenate([-s_, s_], 0)
    inv = (10000.0 ** (-np.arange(0, 32, 2, dtype=f) / 32)).astype(f)
    ang = pos[:, None] * inv[None, :]
    c_, s_ = np.cos(ang).astype(f).T, np.sin(ang).astype(f).T
    rope[0:64, 2] = 1.0
    rope[64:96, 2] = np.concatenate([c_, c_], 0)
    rope[64:96, 3] = np.concatenate([-s_, s_], 0)
    return {"cbf": cbf, "cebig": cebig, "cf32": cf, "rope": rope}


def prep_layer(inp, l):
    f = np.float32
    w_in = inp["w_in"][l]
    Wd = {}
    chs = fm_chunk_cols()
    w_fm = np.zeros((NFM, 128, NFC, 128), f)
    for ci, cols in enumerate(chs):
        w_fm[ci, :, :, :len(cols)] = w_in[:, cols].reshape(NFC, 128, len(cols)).transpose(1, 0, 2)
    Wd["w_fm"] = w_fm.reshape(NFM, 128, D)
    Wd["w_tm"] = np.ascontiguousarray(w_in[:, tm_cols()].reshape(NFC, 128, TMW).transpose(1, 0, 2)).reshape(128, NFC * TMW)
    sv = np.zeros((128, 16), f)
    sv[0:64, 0] = inp["nsa_q_norm"][l]
    for i in range(3):
        sv[0:64, 1 + i] = inp["nsa_k_norm"][l, i]
    sv[:, 4:7] = inp["mla_q_lat_norm"][l].reshape(3, 128).T
    sv[:, 7:9] = inp["mla_kv_lat_norm"][l].reshape(2, 128).T
    sv[0:96, 9] = inp["mla_q_norm"][l]
    sv[0:96, 10] = inp["mla_k_norm"][l]
    Wd["svec"] = sv
    Wd["w_uq"] = np.ascontiguousarray(inp["mla_w_uq"][l].reshape(3, 128, 768).transpose(1, 0, 2)).reshape(128, 3 * 768)
    ukv = inp["mla_w_ukv"][l].reshape(2, 128, 8, 128)
    wk = np.zeros((128, 2, 8, 96), f)
    wk[:, :, :, 0:64] = ukv[:, :, :, 0:64].transpose(1, 0, 2, 3)
    Wd["w_uk"] = wk.reshape(128, 2 * 8 * 96)
    Wd["w_uv"] = np.ascontiguousarray(ukv[:, :, :, 64:128].transpose(1, 0, 2, 3)).reshape(128, 2 * 512)
    cp = np.zeros((128, 4, 36), f)
    cp[:, :, 0:31] = inp["conv_dw_w"][l].T.reshape(4, 128, 31).transpose(1, 0, 2)
    cp[:, :, 31] = inp["conv_dw_b"][l].reshape(4, 128).T
    cp[:, :, 32] = inp["conv_ln_g"][l].reshape(4, 128).T
    cp[:, :, 33] = inp["conv_ln_b"][l].reshape(4, 128).T
    Wd["conv_par"] = cp
    Wd["conv_pw"] = np.ascontiguousarray(inp["conv_pw_w"][l].reshape(4, 128, 512).transpose(1, 0, 2)).reshape(128, 4 * 512)
    Wd["conv_pwb"] = np.ascontiguousarray(np.broadcast_to(inp["conv_pw_b"][l][None, :], (128, 512))).astype(f)
    cw = inp["nsa_cmp_w"][l].reshape(2, 32, 64, 64)
    wkb = np.zeros((128, 32, 128), f)
    for g in range(2):
        wkb[64 * g:64 * g + 64, :, 64 * g:64 * g + 64] = cw[0].transpose(1, 0, 2)
    Wd["cmp_wk"] = wkb.reshape(128, 32 * 128)
    Wd["cmp_wv"] = np.ascontiguousarray(np.concatenate([cw[1].transpose(1, 0, 2)] * 2, 0)).reshape(128, 32 * 64)
    pe = inp["nsa_cmp_pe"][l]
    peT = np.ascontiguousarray(pe.transpose(2, 0, 1)).reshape(64, 64)
    Wd["cmp_peT"] = np.ascontiguousarray(np.concatenate([peT, peT], 0))
    Wd["w_o"] = np.ascontiguousarray(inp["w_o"][l].reshape(NFC, 128, NFC, 128).transpose(2, 1, 0, 3)).reshape(NFC, 128, D)
    Wd["gn"] = vec_fm(inp["group_norm"][l])
    Wd["gain_mix"] = vec_fm(inp["norm_mix"][l])
    Wd["gain_ffn"] = vec_fm(inp["norm_ffn"][l])
    Wd["w_up"] = np.ascontiguousarray(inp["ffn_up"][l].reshape(16, 128, 88, 128).transpose(2, 1, 0, 3)).reshape(88, 128, 2048)
    Wd["w_dn"] = np.ascontiguousarray(inp["ffn_down"][l].reshape(4, 11, 128, 16, 128).transpose(0, 3, 2, 1, 4)).reshape(4, 16, 128, 11 * 128)
    Wd["cw"] = np.ascontiguousarray(inp["ffn_conv_w"][l].T.reshape(88, 128, 3).transpose(1, 0, 2))
    Wd["cb"] = np.ascontiguousarray(inp["ffn_conv_b"][l].reshape(88, 128).T)
    return Wd


A_W = ("w_fm", "w_tm", "svec", "w_uq", "w_uk", "w_uv")
B_W = ("w_o", "conv_par", "conv_pw", "conv_pwb", "cmp_wk", "cmp_wv", "cmp_peT")


def assemble_full(name, pk0, pk1):
    a0, a1 = pk0[name], pk1[name]
    tok_first = name in ("sb_v", "nsa_v", "mla_v")
    if tok_first:
        t0 = a0.reshape(NQT, 128, -1)
        t1 = a1.reshape(NQT, 128, -1)
        return np.ascontiguousarray(np.stack([t0, t1], 1).reshape(2 * NT, -1))
    sh = a0.shape[:-1]
    t0 = a0.reshape(sh + (NQT, 128))
    t1 = a1.reshape(sh + (NQT, 128))
    return np.ascontiguousarray(np.stack([t0, t1], -2).reshape(sh + (2 * NT,)))


_NC_CACHE = {}


def _nc(name, fn):
    if name not in _NC_CACHE:
        _NC_CACHE[name] = fn()
    return _NC_CACHE[name]


def _unfeat(xT):
    p, nf, T = xT.shape
    return np.ascontiguousarray(np.asarray(xT).transpose(1, 0, 2).reshape(nf * p, T).T)


def _run(nc, in_maps):
    res = run_bass_kernel_spmd(nc, in_maps, core_ids=list(range(len(in_maps))))
    return res.results


def compute_mod(inp):
    f = np.float32
    c = np.asarray(inp["c"], f)
    cT = np.ascontiguousarray(c.T.reshape(NFC, 128, 4).transpose(1, 0, 2))
    in_maps = []
    for k in range(8):
        w = np.zeros((2 * NMC, 128, NFC, 128), f)
        bb = np.zeros((128, 2 * NMC), f)
        for l in range(2):
            for q in range(NMC):
                ch = k * NMC + q
                w[l * NMC + q] = inp["ada_w"][l][:, ch * 128:(ch + 1) * 128].reshape(NFC, 128, 128).transpose(1, 0, 2)
                bb[:, l * NMC + q] = inp["ada_b"][l][ch * 128:(ch + 1) * 128]
        in_maps.append({"cT": cT, "ada_w": w.reshape(2 * NMC, 128, D), "ada_b": bb})
    res = _run(_nc("mod", build_launch_mod), in_maps)
    modT = np.zeros((2, 4, 128, 96), f)
    for k in range(8):
        mp = np.asarray(res[k]["modp"])
        for l in range(2):
            for q in range(NMC):
                modT[l, :, :, k * NMC + q] = mp[:, l * NMC + q, :].T
    return modT


def kernel(**inp):
    f = np.float32
    inp = {k: np.asarray(v) for k, v in inp.items()}
    x = inp["x"].astype(f, copy=False)
    B = x.shape[0]
    modT = compute_mod(inp)
    consts = [make_consts(j) for j in range(2)]
    toks = [own_tokens(j) for j in range(2)]
    cores = [(b, j) for b in range(B) for j in range(2)]
    xc = [feat_major(x[b][toks[j]]) for (b, j) in cores]
    for l in range(2):
        Wd = prep_layer(inp, l)
        in_maps = []
        for ci, (b, j) in enumerate(cores):
            m = {"xT": xc[ci], "modT": np.ascontiguousarray(modT[l, b]), "gain": Wd["gain_mix"], **consts[j]}
            for k in A_W:
                m[k] = Wd[k]
            in_maps.append(m)
        ra = _run(_nc("A", build_launch_a), in_maps)
        pks = [{k[2:]: np.asarray(v) for k, v in r.items() if k.startswith("o_")} for r in ra]
        in_maps = []
        for b in range(B):
            p0, p1 = pks[2 * b], pks[2 * b + 1]
            full = {n: assemble_full(n, p0, p1) for n in PK_KV}
            cvf = assemble_full("cv", p0, p1)
            for j in range(2):
                cvh = np.zeros((4, 128, NQT, 30), cvf.dtype)
                for i in range(NQT):
                    g = 2 * i + j
                    if g > 0:
                        cvh[:, :, i, :] = cvf[:, :, g * 128 - 30:g * 128]
                cj = {k: v for k, v in consts[j].items() if k != "rope"}
                m = {"xT": xc[2 * b + j], "modT": np.ascontiguousarray(modT[l, b]), "gn": Wd["gn"], "cvh": cvh, **cj}
                for k in B_W:
                    m[k] = Wd[k]
                for n in PK_OWN:
                    m["i_" + n] = pks[2 * b + j][n]
                for n in PK_KV:
                    m["f_" + n] = full[n]
                in_maps.append(m)
        rb = _run(_nc("B", build_launch_b), in_maps)
        xc = [np.asarray(r["yT"]) for r in rb]
        in_maps = []
        for b in range(B):
            xm = np.zeros((2 * NT, D), f)
            for j in range(2):
                xm[toks[j]] = _unfeat(xc[2 * b + j])
            for j in range(2):
                xh = np.zeros((NQT, 2, D), f)
                hv = np.zeros((NQT, 2), f)
                for i in range(NQT):
                    g = 2 * i + j
                    if g > 0:
                        xh[i] = xm[g * 128 - 2:g * 128]
                        hv[i] = 1.0
                xhT = np.ascontiguousarray(xh.reshape(2 * NQT, D).T.reshape(NFC, 128, 2 * NQT).transpose(1, 0, 2))
                hvT = np.ascontiguousarray(np.broadcast_to(np.tile(hv.reshape(-1), NFC)[None, :], (128, NFC * 16))).astype(f)
                in_maps.append({"xT": xc[2 * b + j], "xh": xhT, "hv": hvT, "modT": np.ascontiguousarray(modT[l, b]), "gain": Wd["gain_ffn"],
                                "w_up": Wd["w_up"], "w_dn": Wd["w_dn"], "cw": Wd["cw"], "cb": Wd["cb"]})
        rc = _run(_nc("C", build_launch_c), in_maps)
        xc = [np.asarray(r["yT"]) for r in rc]
    out = np.zeros((B, 2 * NT, D), f)
    for ci, (b, j) in enumerate(cores):
        out[b][toks[j]] = _unfeat(xc[ci])
    return out
```
